# Optimizing a Trainium2 kernel written in Bass

```python
import jax, jax.numpy as jnp
from jax import lax
import numpy as np

D_MODEL = 1024
BATCH = 8
SEQ = 2048
DEPTH = 2

GRID_W = 64
CTX_LEN = 256
RET_HEADS = 4
RET_DIM = 64
RET_W = RET_HEADS * RET_DIM
RET_CHUNK = 128
MLA_HEADS = 8
MLA_NOPE = 64
MLA_ROPE = 32
MLA_V = 64
MLA_Q_RANK = 256
MLA_KV_RANK = 128
MLA_W = MLA_HEADS * MLA_V
POOL_GROUPS = 4
POOL_WINDOWS = (2, 4, 8, 16)
POOL_W = D_MODEL - RET_W - MLA_W
POOL_GDIM = POOL_W // POOL_GROUPS
MIX_W = RET_W + MLA_W + POOL_W
IN_SIZES = (RET_W, RET_W, RET_W, RET_W, MLA_Q_RANK, MLA_KV_RANK, MLA_ROPE, POOL_W)
IN_W = sum(IN_SIZES)
D_FF = 4 * D_MODEL
Q_BLOCK = 128
ROPE_BASE = 10000.0
EPS = 1e-6

kernel_name = "hybrid_retention_mla_pool_dit"


def rmsnorm(x, g):
    xf = x.astype(jnp.float32)
    y = xf * lax.rsqrt(jnp.mean(xf * xf, axis=-1, keepdims=True) + EPS)
    return (y * g.astype(jnp.float32)).astype(x.dtype)


def modulate(h, shift, scale):
    return h * (1.0 + scale) + shift


def head_norm(o):
    of = o.astype(jnp.float32)
    mu = jnp.mean(of, axis=-1, keepdims=True)
    var = jnp.mean(jnp.square(of - mu), axis=-1, keepdims=True)
    return (of - mu) * lax.rsqrt(var + EPS)


def flip(a):
    return jnp.flip(a, axis=1)


def split_proj(p):
    idx, acc = [], 0
    for s in IN_SIZES[:-1]:
        acc += s
        idx.append(acc)
    return jnp.split(p, idx, axis=-1)


def axial_rope_tables(length, dim):
    rows = length // GRID_W
    row = jnp.repeat(jnp.arange(rows), GRID_W).astype(jnp.float32)
    col = jnp.tile(jnp.arange(GRID_W), rows).astype(jnp.float32)
    n_freq = dim // 4
    inv = ROPE_BASE ** (-jnp.arange(n_freq, dtype=jnp.float32) / n_freq)
    ang = jnp.concatenate([row[:, None] * inv, col[:, None] * inv], axis=-1)
    return jnp.cos(ang), jnp.sin(ang)


def apply_rope(x, cos, sin):
    shape = (x.shape[1],) + (1,) * (x.ndim - 3) + (cos.shape[-1],)
    cos = cos.reshape(shape).astype(x.dtype)
    sin = sin.reshape(shape).astype(x.dtype)
    x1, x2 = jnp.split(x, 2, axis=-1)
    return jnp.concatenate([x1 * cos - x2 * sin, x1 * sin + x2 * cos], axis=-1)


def ret_states(k, v, log_g, s0):
    B, L, H, dk = k.shape
    n = L // RET_CHUNK
    kc = k.reshape(B, n, RET_CHUNK, H, dk)
    vc = v.reshape(B, n, RET_CHUNK, H, -1)
    pos = jnp.arange(RET_CHUNK, dtype=jnp.float32)
    w_k = jnp.exp(log_g[:, None] * (RET_CHUNK - 1.0 - pos)[None, :])
    u = jnp.einsum('bnjhd,hj,bnjhe->nbhde', kc, w_k, vc)
    g_chunk = jnp.exp(log_g * RET_CHUNK)[None, :, None, None]

    def step(s, u_c):
        return g_chunk * s + u_c, s

    s_fin, s_start = lax.scan(step, s0, u)
    return jnp.moveaxis(s_start, 0, 1), s_fin


def ret_outputs(q, k, v, log_g, s_start, inclusive):
    B, L, H, dk = q.shape
    n = L // RET_CHUNK
    qc = q.reshape(B, n, RET_CHUNK, H, dk)
    kc = k.reshape(B, n, RET_CHUNK, H, dk)
    vc = v.reshape(B, n, RET_CHUNK, H, -1)
    pos = jnp.arange(RET_CHUNK, dtype=jnp.float32)
    diff = pos[:, None] - pos[None, :]
    mask = (diff >= 0) if inclusive else (diff > 0)
    d_in = jnp.where(mask[None], jnp.exp(log_g[:, None, None] * jnp.where(mask, diff, 0.0)[None]), 0.0)
    s = jnp.einsum('bnihd,bnjhd->bnhij', qc, kc) * d_in
    o = jnp.einsum('bnhij,bnjhe->bnihe', s, vc)
    w_q = jnp.exp(log_g[:, None] * (pos + 1.0)[None, :])
    o = o + jnp.einsum('bnihd,hi,bnhde->bnihe', qc, w_q, s_start)
    return o.reshape(B, L, H, -1)


def retention_bidir(q, k, v, lg, s0_f, s0_b):
    st_f, fin_f = ret_states(k, v, lg[0], s0_f)
    qb, kb, vb = flip(q), flip(k), flip(v)
    st_b, fin_b = ret_states(kb, vb, lg[1], s0_b)
    o = ret_outputs(q, k, v, lg[0], st_f, True) + flip(ret_outputs(qb, kb, vb, lg[1], st_b, False))
    return o, fin_f, fin_b


def retention_out(o, g):
    B, L = o.shape[:2]
    return jax.nn.silu(g) * head_norm(o).reshape(B, L, RET_W).astype(g.dtype)


def mla_q(cq, q_norm, w_uq):
    B, L, _ = cq.shape
    q = (rmsnorm(cq, q_norm) @ w_uq).reshape(B, L, MLA_HEADS, MLA_NOPE + MLA_ROPE)
    return q[..., :MLA_NOPE], q[..., MLA_NOPE:]


def mla_kv(ckv, kv_norm, w_ukv):
    B, L, _ = ckv.shape
    kv = (rmsnorm(ckv, kv_norm) @ w_ukv).reshape(B, L, MLA_HEADS, MLA_NOPE + MLA_V)
    return kv[..., :MLA_NOPE], kv[..., MLA_NOPE:]


def mla_attend(q_nope, q_pe, k_nope, k_pe, v):
    scale = (MLA_NOPE + MLA_ROPE) ** -0.5
    s = (jnp.einsum('bqhd,bkhd->bhqk', q_nope, k_nope)
         + jnp.einsum('bqhr,bkr->bhqk', q_pe, k_pe)) * scale
    p = jax.nn.softmax(s.astype(jnp.float32), axis=-1).astype(v.dtype)
    return jnp.einsum('bhqk,bkhe->bqhe', p, v)


def blocked_attend(q_nope, q_pe, k_nope, k_pe, v):
    B, L, H, _ = q_nope.shape
    nb = L // Q_BLOCK
    qn = jnp.moveaxis(q_nope.reshape(B, nb, Q_BLOCK, H, -1), 1, 0)
    qp = jnp.moveaxis(q_pe.reshape(B, nb, Q_BLOCK, H, -1), 1, 0)
    o = lax.map(lambda a: mla_attend(a[0], a[1], k_nope, k_pe, v), (qn, qp))
    return jnp.moveaxis(o, 0, 1).reshape(B, L, H, -1)


def pool_branch(u, w_pool, pool_scale):
    B, L, C = u.shape
    win = jnp.repeat(jnp.asarray(POOL_WINDOWS, jnp.int32), POOL_GDIM)
    t = jnp.arange(L, dtype=jnp.int32)[:, None]
    lo = jnp.clip(t - win // 2, 0, L)
    hi = jnp.clip(t - win // 2 + win, 0, L)
    cs = jnp.pad(jnp.cumsum(u.astype(jnp.float32), axis=1), ((0, 0), (1, 0), (0, 0)))
    tot = (jnp.take_along_axis(cs, jnp.broadcast_to(hi[None], (B, L, C)), axis=1)
           - jnp.take_along_axis(cs, jnp.broadcast_to(lo[None], (B, L, C)), axis=1))
    pooled = (tot / (hi - lo).astype(jnp.float32) - u.astype(jnp.float32)).astype(u.dtype)
    y = jnp.einsum('blgc,gcd->blgd', pooled.reshape(B, L, POOL_GROUPS, POOL_GDIM), w_pool)
    return y.reshape(B, L, C) * pool_scale


def token_mix(hx, hc, w_in, q_norm, w_uq, kv_norm, w_ukv, decay_logit, w_pool, pool_scale, w_out, need_ctx):
    B, L, _ = hx.shape
    rq_l, rk_l, rv_l, rg_l, cq_l, ckv_l, kpe_l, u_l = split_proj(hx @ w_in)
    rq_c, rk_c, rv_c, rg_c, cq_c, ckv_c, kpe_c, u_c = split_proj(hc @ w_in)

    def ret_heads(a):
        return a.reshape(a.shape[0], a.shape[1], RET_HEADS, RET_DIM)
    k_scale = RET_DIM ** -0.5
    cos_r, sin_r = axial_rope_tables(L, RET_DIM)
    ql = apply_rope(ret_heads(rq_l), cos_r, sin_r)
    kl = apply_rope(ret_heads(rk_l), cos_r, sin_r) * k_scale
    vl = ret_heads(rv_l)
    qc, kc, vc = ret_heads(rq_c), ret_heads(rk_c) * k_scale, ret_heads(rv_c)
    lg = jax.nn.log_sigmoid(decay_logit.astype(jnp.float32))
    s_zero = jnp.zeros((B, RET_HEADS, RET_DIM, RET_DIM), jnp.float32)
    if need_ctx:
        o_rc, fin_f, fin_b = retention_bidir(qc, kc, vc, lg, s_zero, s_zero)
    else:
        _, fin_f = ret_states(kc, vc, lg[0], s_zero)
        _, fin_b = ret_states(flip(kc), flip(vc), lg[1], s_zero)
    o_rl, _, _ = retention_bidir(ql, kl, vl, lg, fin_f, fin_b)
    ret_l = retention_out(o_rl, rg_l)

    cos_m, sin_m = axial_rope_tables(L, MLA_ROPE)
    qn_l, qp_l = mla_q(cq_l, q_norm, w_uq)
    qp_l = apply_rope(qp_l, cos_m, sin_m)
    kn_l, v_l = mla_kv(ckv_l, kv_norm, w_ukv)
    kpe_l = apply_rope(kpe_l, cos_m, sin_m)
    kn_c, v_c = mla_kv(ckv_c, kv_norm, w_ukv)
    kn_all = jnp.concatenate([kn_c, kn_l], axis=1)
    kpe_all = jnp.concatenate([kpe_c, kpe_l], axis=1)
    v_all = jnp.concatenate([v_c, v_l], axis=1)
    mla_l = blocked_attend(qn_l, qp_l, kn_all, kpe_all, v_all).reshape(B, L, MLA_W)

    pool_l = pool_branch(u_l, w_pool, pool_scale)

    out_l = jnp.concatenate([ret_l, mla_l, pool_l], axis=-1) @ w_out
    if not need_ctx:
        return out_l, None
    Lc = hc.shape[1]
    qn_c, qp_c = mla_q(cq_c, q_norm, w_uq)
    mla_c = mla_attend(qn_c, qp_c, kn_c, kpe_c, v_c).reshape(B, Lc, MLA_W)
    out_c = jnp.concatenate([retention_out(o_rc, rg_c), mla_c, pool_branch(u_c, w_pool, pool_scale)], axis=-1) @ w_out
    return out_l, out_c


def sq_relu_mlp(h, w1, w2):
    return jnp.square(jax.nn.relu(h @ w1)) @ w2


def setup_inputs(seed: int = 0) -> dict:
    key = jax.random.key(seed)
    ks = jax.random.split(key, 20)
    f32 = jnp.float32

    def nrm(k, shape, fan_in):
        return jax.random.normal(k, shape, f32) * (fan_in ** -0.5)

    def gain(k, shape):
        return 1.0 + 0.02 * jax.random.normal(k, shape, f32)

    base = 1.0 - 2.0 ** (-5.0 - jnp.arange(RET_HEADS, dtype=f32))
    logit = jnp.log(base) - jnp.log1p(-base)
    ret_decay_logit = jnp.broadcast_to(logit, (DEPTH, 2, RET_HEADS)) + 0.1 * jax.random.normal(ks[12], (DEPTH, 2, RET_HEADS), f32)
    return {
        "x": jax.random.normal(ks[0], (BATCH, SEQ, D_MODEL), f32),
        "c": jax.random.normal(ks[1], (BATCH, D_MODEL), f32),
        "ctx": jax.random.normal(ks[2], (BATCH, CTX_LEN, D_MODEL), f32),
        "c_ctx": jax.random.normal(ks[3], (D_MODEL,), f32),
        "w_ada": nrm(ks[4], (DEPTH, D_MODEL, 6 * D_MODEL), D_MODEL),
        "b_ada": 0.01 * jax.random.normal(ks[5], (DEPTH, 6 * D_MODEL), f32),
        "norm_mix": gain(ks[6], (DEPTH, D_MODEL)),
        "w_in": nrm(ks[7], (DEPTH, D_MODEL, IN_W), D_MODEL),
        "q_norm": gain(ks[8], (DEPTH, MLA_Q_RANK)),
        "w_uq": nrm(ks[9], (DEPTH, MLA_Q_RANK, MLA_HEADS * (MLA_NOPE + MLA_ROPE)), MLA_Q_RANK),
        "kv_norm": gain(ks[10], (DEPTH, MLA_KV_RANK)),
        "w_ukv": nrm(ks[11], (DEPTH, MLA_KV_RANK, MLA_HEADS * (MLA_NOPE + MLA_V)), MLA_KV_RANK),
        "ret_decay_logit": ret_decay_logit,
        "w_pool": nrm(ks[13], (DEPTH, POOL_GROUPS, POOL_GDIM, POOL_GDIM), POOL_GDIM),
        "pool_scale": 1.0 + 0.1 * jax.random.normal(ks[14], (DEPTH, POOL_W), f32),
        "w_out": nrm(ks[15], (DEPTH, MIX_W, D_MODEL), MIX_W),
        "norm_mlp": gain(ks[16], (DEPTH, D_MODEL)),
        "w_ff1": nrm(ks[17], (DEPTH, D_MODEL, D_FF), D_MODEL),
        "w_ff2": nrm(ks[18], (DEPTH, D_FF, D_MODEL), D_FF),
        "norm_final": gain(ks[19], (D_MODEL,)),
    }


def reference(x, c, ctx, c_ctx, w_ada, b_ada, norm_mix, w_in, q_norm, w_uq, kv_norm, w_ukv,
              ret_decay_logit, w_pool, pool_scale, w_out, norm_mlp, w_ff1, w_ff2, norm_final):
    h = ctx
    for l in range(DEPTH):
        last = l == DEPTH - 1
        mod_x = (jax.nn.silu(c) @ w_ada[l] + b_ada[l])[:, None, :]
        mod_c = jax.nn.silu(c_ctx) @ w_ada[l] + b_ada[l]
        sh1, sc1, g1, sh2, sc2, g2 = jnp.split(mod_x, 6, axis=-1)
        csh1, csc1, cg1, csh2, csc2, cg2 = jnp.split(mod_c, 6, axis=-1)

        hx = modulate(rmsnorm(x, norm_mix[l]), sh1, sc1)
        hc = modulate(rmsnorm(h, norm_mix[l]), csh1, csc1)
        ox, oc = token_mix(hx, hc, w_in[l], q_norm[l], w_uq[l], kv_norm[l], w_ukv[l],
                           ret_decay_logit[l], w_pool[l], pool_scale[l], w_out[l], not last)
        x = x + g1 * ox
        x = x + g2 * sq_relu_mlp(modulate(rmsnorm(x, norm_mlp[l]), sh2, sc2), w_ff1[l], w_ff2[l])
        if not last:
            h = h + cg1 * oc
            h = h + cg2 * sq_relu_mlp(modulate(rmsnorm(h, norm_mlp[l]), csh2, csc2), w_ff1[l], w_ff2[l])
    return rmsnorm(x, norm_final)
```

```python
import numpy as np
from contextlib import ExitStack
import concourse.bass as bass
import concourse.mybir as mybir
from concourse.bass_utils import run_bass_kernel_spmd

F32 = mybir.dt.float32
BF16 = mybir.dt.bfloat16
ALU = mybir.AluOpType
AF = mybir.ActivationFunctionType
AX = mybir.AxisListType

ENGS = ("pe", "act", "dve", "pool", "sp")
D = 1024
T = 2304
NT = 18
BLK = [(0, 256), (256, 512), (768, 512), (1280, 512), (1792, 512)]
EPS = 1e-6
SCALE_MLA = 96.0 ** -0.5


class _Op:
    __slots__ = ("fn", "waits", "milestone", "seen", "dma_sem")

    def __init__(self, fn, waits, seen, dma_sem=None):
        self.fn = fn
        self.waits = waits
        self.milestone = False
        self.seen = seen
        self.dma_sem = dma_sem


class _Region:
    __slots__ = ("writer", "readers")

    def __init__(self):
        self.writer = None
        self.readers = {}


class FW:
    N_DMA_SEMS = 24

    def __init__(self, nc):
        self.nc = nc
        self.ops = {e: [] for e in ENGS}
        self.seen = {e: {} for e in ENGS}
        self.regs = {}
        self.dma_cnt = {}
        self.dma_rr = {"sp": 0, "pool": 0}

    def region(self, key):
        r = self.regs.get(key)
        if r is None:
            r = self.regs[key] = _Region()
        return r

    def _record(self, eng, fn, reads, writes, extra=None, dma_sem=None):
        deps = dict(extra) if extra else {}

        def add(k, v):
            if deps.get(k, -1) < v:
                deps[k] = v

        for key in reads:
            R = self.region(key)
            if R.writer is not None:
                add(*R.writer)
            if isinstance(key, str) and (key == "psT" or (key[0] == "b" and key[1:].isdigit())):
                for k, v in R.readers.items():
                    if k != eng:
                        add(k, v)
        for key in writes:
            R = self.region(key)
            if R.writer is not None:
                add(*R.writer)
            for k, v in R.readers.items():
                add(k, v)
        idx = len(self.ops[eng])
        seen = self.seen[eng]
        waits = {}
        for k, v in deps.items():
            if k == eng and eng in ("pe", "sp"):
                continue
            if seen.get(k, -1) >= v:
                continue
            waits[k] = v
            seen[k] = v
            if k in ENGS:
                tgt = self.ops[k][v]
                tgt.milestone = True
                for e2, v2 in zip(ENGS, tgt.seen):
                    if seen.get(e2, -1) < v2:
                        seen[e2] = v2
        snap = tuple(idx if e == eng else seen.get(e, -1) for e in ENGS)
        self.ops[eng].append(_Op(fn, waits, snap, dma_sem))
        return idx

    def op(self, eng, fn, reads=(), writes=()):
        idx = self._record(eng, fn, reads, writes)
        for key in reads:
            self.region(key).readers[eng] = idx
        for key in writes:
            R = self.region(key)
            R.writer = (eng, idx)
            R.readers = {}
        return idx

    def dma(self, queue, out, in_, reads=(), writes=()):
        i = self.dma_rr[queue]
        self.dma_rr[queue] = (i + 1) % self.N_DMA_SEMS
        skey = ("d", queue, i)
        prev = self.dma_cnt.get(skey, 0)
        val = prev + 16
        self.dma_cnt[skey] = val
        extra = {skey: prev} if prev > 0 else None

        def fn(e, out=out, in_=in_):
            return e.dma_start(out=out, in_=in_)

        self._record(queue, fn, reads, writes, extra=extra, dma_sem=skey)
        for key in reads:
            self.region(key).readers[skey] = val
        for key in writes:
            R = self.region(key)
            R.writer = (skey, val)
            R.readers = {}

    def barrier(self):
        deps = {}
        for e in ("pe", "act", "dve", "pool"):
            lst = self.ops[e]
            for i in range(len(lst) - 1, -1, -1):
                if lst[i].fn is not None and lst[i].dma_sem is None:
                    deps[e] = i
                    break
        for k, v in self.dma_cnt.items():
            deps[k] = v
        for e in ENGS:
            ex = {k: v for k, v in deps.items() if k != e}
            self._record(e, None, (), (), extra=ex)

    def finish(self, eng, keys):
        self._record(eng, None, keys, ())

    def emit(self):
        nc = self.nc
        counts = {}
        for e in ENGS:
            c = 0
            lst = []
            for o in self.ops[e]:
                if o.milestone:
                    assert o.fn is not None and o.dma_sem is None
                    c += 1
                lst.append(c)
            counts[e] = lst
        with ExitStack() as st:
            sems = {e: st.enter_context(nc.semaphore("s_" + e)) for e in ENGS}
            for k in self.dma_cnt:
                sems[k] = st.enter_context(nc.semaphore("d_%s_%d" % (k[1], k[2])))
            block = st.enter_context(nc.Block())

            def run(ename, e):
                for o in self.ops[ename]:
                    for k, v in o.waits.items():
                        if k in ENGS:
                            e.wait_ge(sems[k], counts[k][v])
                        else:
                            e.wait_ge(sems[k], v)
                    if o.fn is None:
                        continue
                    ins = o.fn(e)
                    if o.dma_sem is not None:
                        ins.then_inc(sems[o.dma_sem], 16)
                    elif o.milestone:
                        ins.then_inc(sems[ename], 1)

            @block.tensor
            def _(e):
                run("pe", e)

            @block.scalar
            def _(e):
                run("act", e)

            @block.vector
            def _(e):
                run("dve", e)

            @block.gpsimd
            def _(e):
                run("pool", e)

            @block.sync
            def _(e):
                run("sp", e)


DBG = {}


class Arena:
    def __init__(self, nc, fw, base, limit):
        self.nc, self.fw, self.off, self.limit = nc, fw, base, limit
        self.n = 0

    def alloc(self, name, shape, dt):
        esz = 2 if dt == BF16 else 4
        sz = esz
        for s in shape[1:]:
            sz *= s
        sz = (sz + 63) // 64 * 64
        off = self.off
        assert off + sz <= self.limit, "SBUF arena overflow: %s needs %d at %d (limit %d)" % (name, sz, off, self.limit)
        self.off += sz
        self.n += 1
        t = self.nc.alloc_sbuf_tensor_at("%s_%d" % (name, self.n), list(shape), dt, offset=off)
        DBG[name] = t
        return t

    def alloc_at(self, name, shape, dt, off):
        self.n += 1
        t = self.nc.alloc_sbuf_tensor_at("%s_%d" % (name, self.n), list(shape), dt, offset=off)
        DBG[name] = t
        return t

    def mark(self):
        return self.off

    def release(self, m):
        self.fw.barrier()
        self.off = m


class _Stop(Exception):
    pass


def build_nc(stage=None):
    import os
    if stage is None:
        stage = int(os.environ.get('KSTAGE', '99'))
    nc = bass.Bass("TRN2", target_bir_lowering=False)
    fw = FW(nc)

    used_inputs = set()
    nc._used_inputs = used_inputs

    class LazyAP:
        def __init__(self, name, shape):
            self.name, self.shape_, self._t = name, list(shape), None

        def ap(self):
            if self._t is None:
                self._t = nc.dram_tensor(self.name, self.shape_, F32, kind="ExternalInput").ap()
                used_inputs.add(self.name)
            return self._t

        def __getitem__(self, k):
            return self.ap()[k]

        def rearrange(self, *a, **k):
            return self.ap().rearrange(*a, **k)

    def din(name, shape):
        return LazyAP(name, shape)

    _odma = fw.dma

    def _dma(queue, out, in_, reads=(), writes=()):
        if isinstance(in_, LazyAP):
            in_ = in_.ap()
        return _odma(queue, out, in_, reads=reads, writes=writes)

    fw.dma = _dma

    xT_d = din("xT", [D, T])
    cc_d = din("cc", [128, 16])
    w_ada_d = din("w_ada", [2, D, 6144])
    b_ada_d = din("b_adaT", [128, 96])
    gmix_d = din("gmix", [128, 16])
    gmlp_d = din("gmlp", [128, 16])
    gfin_d = din("gfin", [128, 8])
    w_in_d = din("w_in_ext", [2, D, 1728])
    qn_d = din("qnorm", [128, 4])
    kvn_d = din("kvnorm", [128, 2])
    w_uq_d = din("w_uq_ext", [2, 256, 1536])
    w_ukv_d = din("w_ukv", [2, 128, 1024])
    logit_d = din("logit", [128, 16])
    w_pool_d = din("w_pool", [2, 4, 64, 64])
    psc_d = din("pscale", [64, 8])
    w_out_d = din("w_out", [2, D, D])
    w_ff1_d = din("w_ff1", [2, D, 4096])
    w_ff2_d = din("w_ff2", [2, 4096, D])
    cosD_d = din("cosD", [128, 16 * 64])
    sinS_d = din("sinS", [128, 16 * 64])
    ropeM_d = din("ropeM", [128, 2 * 2048])
    bands_d = din("bands", [128, 20 * 128])
    mconst_d = din("mconst", [128, 4 * 128])
    efree_d = din("efree", [128, 2 * 128])
    ecol_d = din("ecol", [128, 2])
    outT_d = nc.dram_tensor("outT", [D, 2048], F32, kind="ExternalOutput").ap()

    ps2 = [nc.alloc_psum_tensor("ps2_%d" % i, [128, 1024], F32) for i in range(2)]
    ps1 = [nc.alloc_psum_tensor("ps1_%d" % i, [128, 512], F32) for i in range(3)]
    psT = nc.alloc_psum_tensor("psT", [128, 1024], BF16)
    gen_banks = [(ps2[0], 0, "b0"), (ps2[0], 512, "b1"), (ps2[1], 0, "b2"), (ps2[1], 512, "b3"),
                 (ps1[0], 0, "b4"), (ps1[1], 0, "b5"), (ps1[2], 0, "b6")]
    rr = {"i": 0}

    class Bank:
        def __init__(self, t, off, key):
            self.t, self.off, self.key = t, off, key

        def ap(self, p0, p1, c0, c1):
            return self.t[p0:p1, self.off + c0:self.off + c1]

    psT_f32 = psT[:, :].bitcast(F32)

    class BankAP:
        def __init__(self, ap2, key):
            self.a, self.key = ap2, key

        def ap(self, p0, p1, c0, c1):
            return self.a[p0:p1, c0:c1]

    def next_bank(sel=None):
        lst = gen_banks if sel is None else [gen_banks[i] if i < 7 else None for i in sel]
        b = lst[rr["i"] % len(lst)]
        rr["i"] += 1
        if b is None:
            return BankAP(psT_f32, "psT")
        return Bank(*b)

    TOTAL = 212000
    ar = Arena(nc, fw, 16512, 229344)
    X = ar.alloc("X", [128, 8, T], F32)
    H_OFF = ar.off
    H = ar.alloc("H", [128, 8, T], BF16)
    ident = ar.alloc("ident", [128, 128], BF16)
    ones = ar.alloc("ones", [128, 128], BF16)
    onesf = ar.alloc("onesf", [128, 64], F32)
    cc = ar.alloc("cc", [128, 8, 2], F32)
    silc = ar.alloc("silc", [128, 8, 2], BF16)
    bT = ar.alloc("bT", [128, 2, 48], F32)
    modT = ar.alloc("modT", [128, 2, 48, 2], F32)
    gmix = ar.alloc("gmix", [128, 2, 8], F32)
    gmlp = ar.alloc("gmlp", [128, 2, 8], F32)
    gfin = ar.alloc("gfin", [128, 8], F32)
    A1 = ar.alloc("A1", [128, 2, 8, 2], F32)
    A2 = ar.alloc("A2", [128, 2, 8, 2], F32)
    AF_ = ar.alloc("AFin", [128, 8], F32)
    qn = ar.alloc("qn", [128, 2, 2], F32)
    kvn = ar.alloc("kvn", [128, 2], F32)
    psc = ar.alloc("psc", [64, 2, 4], F32)
    lgt = ar.alloc("lgt", [128, 16], F32)
    lg = ar.alloc("lg", [128, 2, 2, 4], F32)
    ecol = ar.alloc("ecol", [128, 2], F32)
    small = ar.alloc("small", [128, 64], F32)
    epsb = ar.alloc("epsb", [128, 4], F32)

    def xk(c, bi):
        return ("X", c, bi)

    def hk(bi):
        return ("H", bi)

    def MM(out, lhsT, rhs, start, stop, r, w):
        fw.op("pe", lambda e: e.matmul(out, lhsT, rhs, start=start, stop=stop), r, w)

    def TR(out, in_, r, w):
        p = in_.shape[0]
        fw.op("pe", lambda e: e.transpose(out, in_, ident[0:p, 0:p]), list(r) + ["ident"], w)

    def ACT(out, in_, func, r, w, bias=None, scale=None):
        kw = {}
        if bias is not None:
            kw["bias"] = bias
        if scale is not None:
            kw["scale"] = scale
        fw.op("act", lambda e: e.activation(out=out, in_=in_, func=func, **kw), r, w)

    def TT(out, in0, in1, op, r, w, eng="dve"):
        fw.op(eng, lambda e: e.tensor_tensor(out=out, in0=in0, in1=in1, op=op), r, w)

    def TS(out, in0, s1, s2, op0, op1, r, w, eng="dve"):
        if op1 is None:
            fw.op(eng, lambda e: e.tensor_scalar(out=out, in0=in0, scalar1=s1, scalar2=None, op0=op0), r, w)
        else:
            fw.op(eng, lambda e: e.tensor_scalar(out=out, in0=in0, scalar1=s1, scalar2=s2, op0=op0, op1=op1), r, w)

    def STT(out, in0, scalar, in1, op0, op1, r, w, eng="dve"):
        fw.op(eng, lambda e: e.scalar_tensor_tensor(out=out, in0=in0, scalar=scalar, in1=in1, op0=op0, op1=op1), r, w)

    def CP(out, in_, r, w, eng="dve"):
        if eng == "act":
            ACT(out, in_, AF.Copy, r, w)
        else:
            fw.op(eng, lambda e: e.tensor_copy(out=out, in_=in_), r, w)

    def RSUM(out, in_, r, w):
        fw.op("dve", lambda e: e.reduce_sum(out=out, in_=in_, axis=AX.X), r, w)

    def RECIP(out, in_, r, w):
        fw.op("dve", lambda e: e.reciprocal(out=out, in_=in_), r, w)

    def MEMSET(ap, val, w, eng="pool"):
        fw.op(eng, lambda e: e.memset(ap, val), (), w)

    def bc(ap, axis, shape):
        return ap.unsqueeze(axis).broadcast_to(list(shape))

    for c in range(8):
        fw.dma("sp", X[:, c, :], xT_d[c * 128:(c + 1) * 128, :], writes=[xk(c, bi) for bi in range(5)])
    fw.dma("sp", cc[:, :, :], cc_d.rearrange("p (k w) -> p k w", w=2), writes=["cc"])
    fw.dma("sp", bT[:, :, :], b_ada_d.rearrange("p (l m) -> p l m", l=2), writes=["bT"])
    fw.dma("sp", gmix[:, :, :], gmix_d.rearrange("p (l c) -> p l c", l=2), writes=["gmix"])
    fw.dma("sp", gmlp[:, :, :], gmlp_d.rearrange("p (l c) -> p l c", l=2), writes=["gmlp"])
    fw.dma("sp", gfin[:, :], gfin_d, writes=["gfin"])
    fw.dma("sp", qn[:, :, :], qn_d.rearrange("p (l c) -> p l c", l=2), writes=["qn"])
    fw.dma("sp", kvn[:, :], kvn_d, writes=["kvn"])
    fw.dma("sp", psc[:, :, :], psc_d.rearrange("p (l g) -> p l g", l=2), writes=["psc"])
    fw.dma("sp", lgt[:, :], logit_d, writes=["lgt"])
    fw.dma("sp", ecol[:, :], ecol_d, writes=["ecol"])
    MEMSET(ident[:, :], 0.0, ["ident"])
    fw.op("pool", lambda e: e.affine_select(out=ident[:, :], in_=ident[:, :], pattern=[[-1, 128]],
                                            compare_op=ALU.not_equal, fill=1.0, base=0, channel_multiplier=1),
          ["ident"], ["ident"])
    MEMSET(ones[:, :], 1.0, ["ones"])
    MEMSET(onesf[:, :], 1.0, ["onesf"])
    MEMSET(epsb[:, 0:1], 1024.0 * EPS, ["epsb"])
    MEMSET(epsb[:, 1:2], EPS, ["epsb"])
    MEMSET(epsb[:, 2:3], 1.0, ["epsb"])
    MEMSET(epsb[:, 3:4], 0.0, ["epsb"])

    ccf = cc[:, :, :].rearrange("p k w -> p (k w)")
    ACT(small[:, 0:16], ccf, AF.Exp, ["cc"], ["small"], scale=-1.0)
    TS(small[:, 0:16], small[:, 0:16], 1.0, None, ALU.add, None, ["small"], ["small"])
    RECIP(small[:, 0:16], small[:, 0:16], ["small"], ["small"])
    TT(silc[:, :, :].rearrange("p k w -> p (k w)"), ccf, small[:, 0:16], ALU.mult, ["cc", "small"], ["silc"])
    lgf = lg[:, :, :, :].rearrange("p l d h -> p (l d h)")
    ACT(small[:, 16:32], lgt[:, :], AF.Exp, ["lgt"], ["small2"], scale=-1.0)
    ACT(small[:, 16:32], small[:, 16:32], AF.Ln, ["small2", "epsb"], ["small2"], bias=epsb[:, 2:3])
    TS(lgf, small[:, 16:32], -1.0, None, ALU.mult, None, ["small2"], ["lg"])

    TS(AF_[:, :], gfin[:, :], 32.0, None, ALU.mult, None, ["gfin"], ["AFin"])
    persist_mark = ar.mark()
    ada_cnt = {"i": 0}

    def adaln_piece(l, j, wada, bankidx, alias=()):
        i = ada_cnt["i"]
        ada_cnt["i"] += 1
        slot = wada[i % 2]
        sk = ("wada", i % 2)
        bank = next_bank([bankidx])
        fw.dma("pool", slot[:, :, :],
               w_ada_d[l].rearrange("(k p) n -> p k n", p=128)[:, :, j * 1024:(j + 1) * 1024], writes=[sk] + list(alias))
        for m in range(8):
            for k in range(8):
                MM(bank.ap(0, 128, m * 2, m * 2 + 2), slot[:, k, m * 128:(m + 1) * 128], silc[:, k, :],
                   k == 0, k == 7, [sk, "silc"], [bank.key])
        TT(modT[:, l, j * 8:(j + 1) * 8, :], bank.ap(0, 128, 0, 16).rearrange("p (m w) -> p m w", w=2),
           bc(bT[:, l, j * 8:(j + 1) * 8], 2, [128, 8, 2]), ALU.add, [bank.key, "bT"], [("modT", l, j)])

    def adaln_A(l, which):
        A, g, vi = ((A1, gmix, 1), (A2, gmlp, 4))[which]
        TS(A[:, l, :, :], modT[:, l, vi * 8:(vi + 1) * 8, :], 1.0, 32.0, ALU.add, ALU.mult, [("modT", l, vi)], [("A", l, which)])
        TT(A[:, l, :, :], A[:, l, :, :], bc(g[:, l, :], 2, [128, 8, 2]), ALU.mult, [("A", l, which), "gmix", "gmlp"],
           [("A", l, which)])

    def mod(l, vi, c, w):
        return modT[:, l, vi * 8 + c, w:w + 1]

    def norm_pass(l, A_of, sh_of, blocks, dst, tmp_pool, A_keys):
        sq, rs, tmpf = tmp_pool
        for bi in blocks:
            t0, n = BLK[bi]
            w = 1 if bi == 0 else 0
            s = sq[bi % 2]
            sk = ("sq", bi % 2)
            ACT(s[:, :, 0:n], X[:, :, t0:t0 + n], AF.Square, [xk(c, bi) for c in range(8)], [sk])
            bank = next_bank()
            for c in range(8):
                MM(bank.ap(0, 128, 0, n), ones[:, :], s[:, c, 0:n], c == 0, c == 7, [sk, "ones"], [bank.key])
            r_ = rs[bi % 2]
            rk_ = ("rs", bi % 2)
            ACT(r_[:, 0:n], bank.ap(0, 128, 0, n), AF.Ln, [bank.key, "epsb"], [rk_], bias=epsb[:, 0:1])
            ACT(r_[:, 0:n], r_[:, 0:n], AF.Exp, [rk_], [rk_], scale=-0.5)
            for c in range(8):
                out_ap, out_key, shv = dst(c, bi, t0, n)
                if shv is None:
                    STT(out_ap, X[:, c, t0:t0 + n], A_of(c, w), r_[:, 0:n], ALU.mult, ALU.mult,
                        [xk(c, bi), rk_] + A_keys, [out_key])
                else:
                    tf = tmpf[c % 2]
                    tk = ("tmpf", c % 2)
                    STT(tf[:, 0:n], X[:, c, t0:t0 + n], A_of(c, w), r_[:, 0:n], ALU.mult, ALU.mult,
                        [xk(c, bi), rk_] + A_keys, [tk])
                    ACT(out_ap, tf[:, 0:n], AF.Identity, [tk] + A_keys, [out_key], bias=shv(c, w))

    def xupdate(l, gvi, oc, bi, t0, n, bank):
        w = 1 if bi == 0 else 0
        STT(X[:, oc, t0:t0 + n], bank.ap(0, 128, 0, n), mod(l, gvi, oc, w), X[:, oc, t0:t0 + n],
            ALU.mult, ALU.add, [bank.key, ("modT", l, gvi), xk(oc, bi)], [xk(oc, bi)])

    B_ORDER = [1, 0] + list(range(17, 1, -1))
    fin_state = {"cnt": 0, "outs": [], "done": False}

    def final_norm_block(bi, sq, rs, ost):
        t0, n = BLK[bi]
        s_ = sq[bi % 2]
        sk = ("sq", bi % 2)
        ACT(s_[:, :, 0:n], X[:, :, t0:t0 + n], AF.Square, [xk(c, bi) for c in range(8)], [sk])
        bank = next_bank()
        for c in range(8):
            MM(bank.ap(0, 128, 0, n), ones[:, :], s_[:, c, 0:n], c == 0, c == 7, [sk, "ones"], [bank.key])
        r_ = rs[bi % 2]
        rk_ = ("rs", bi % 2)
        ACT(r_[:, 0:n], bank.ap(0, 128, 0, n), AF.Ln, [bank.key, "epsb"], [rk_], bias=epsb[:, 0:1])
        ACT(r_[:, 0:n], r_[:, 0:n], AF.Exp, [rk_], [rk_], scale=-0.5)
        for c in range(8):
            i = fin_state["cnt"] % 4
            fin_state["cnt"] += 1
            STT(ost[i][:, 0:n], X[:, c, t0:t0 + n], AF_[:, c:c + 1], r_[:, 0:n], ALU.mult, ALU.mult,
                [xk(c, bi), rk_, "AFin"], [("ost", i)])
            ok = ("out", c, bi)
            fw.dma("sp", outT_d[c * 128:(c + 1) * 128, t0 - 256:t0 - 256 + n], ost[i][:, 0:n], reads=[("ost", i)], writes=[ok])
            fin_state["outs"].append(ok)


    def chk(k, l):
        if stage < k + 10 * l:
            raise _Stop()

    try:
      for l in range(2):
          qblocks = list(range(5)) if l == 0 else [1, 2, 3, 4]
          chk(2, l)
          def _p0():
              m0 = ar.mark()
              if l == 0:
                  wada0 = [ar.alloc("wada", [128, 8, 1024], BF16) for _ in range(2)]
                  adaln_piece(0, 0, wada0, 4)
                  adaln_piece(0, 1, wada0, 5)
                  adaln_A(0, 0)
              sq = [ar.alloc("sq", [128, 8, 512], BF16) for _ in range(2)]
              rs = [ar.alloc("rs", [128, 512], F32) for _ in range(2)]
              tmpf = [ar.alloc("tmpf", [128, 512], F32) for _ in range(2)]
              norm_pass(l, lambda c, w: A1[:, l, c, w:w + 1], lambda c, w: mod(l, 0, c, w), range(5),
                        lambda c, bi, t0, n: (H[:, c, t0:t0 + n], hk(bi), (lambda c_, w_: mod(l, 0, c_, w_))),
                        (sq, rs, tmpf), [("A", l, 0), ("modT", l, 0)])
              if l == 0:
                  for j in range(2, 6):
                      adaln_piece(0, j, wada0, 4 + j % 2)
                  adaln_A(0, 1)
              ar.release(m0)
          if l == 0:
              _p0()

          m0 = ar.mark()
          Wret = ar.alloc("Wret", [128, 8, 1024], BF16)
          fw.dma("pool", Wret[:, :, :], w_in_d[l].rearrange("(k p) n -> p k n", p=128)[:, :, 0:1024], writes=["Wret"])
          Wout_r = ar.alloc("Wout_r", [128, 2, 1024], BF16)
          fw.dma("pool", Wout_r[:, :, :], w_out_d[l, 0:256, :].rearrange("(k p) n -> p k n", p=128), writes=["Wout_r"])
          cosD = ar.alloc("cosD", [128, 16, 64], F32)
          sinS = ar.alloc("sinS", [128, 16, 64], F32)
          fw.dma("sp", cosD[:, :, :], cosD_d.rearrange("p (n d) -> p n d", d=64), writes=["cosD"])
          fw.dma("sp", sinS[:, :, :], sinS_d.rearrange("p (n d) -> p n d", d=64), writes=["sinS"])
          mconst = ar.alloc("mconst", [128, 4, 128], F32)
          fw.dma("sp", mconst[:, :, :], mconst_d.rearrange("p (a i) -> p a i", a=4), writes=["mconst"])
          efree = ar.alloc("efree", [128, 2, 128], F32)
          fw.dma("sp", efree[:, :, :], efree_d.rearrange("p (a i) -> p a i", a=2), writes=["efree"])
          Mtab = ar.alloc("Mtab", [128, 4, 128], F32)
          mtmp = ar.alloc("mtmp", [128, 2, 128], F32)
          tabQ = ar.alloc("tabQ", [128, 2, 4, 128], F32)
          wk = ar.alloc("wk", [128, 2, 4], F32)
          G = ar.alloc("G", [128, 2, 4], F32)
          snapB = ar.alloc("snapB", [64, NT, 256], BF16)
          Sst = ar.alloc("Sst", [64, 256], F32)
          Sbf = ar.alloc("Sbf", [64, 256], BF16)
          for d_ in range(2):
              ACT(wk[:, d_, :], lg[:, l, d_, :], AF.Exp, ["lg", "ecol"], ["wk"], scale=ecol[:, d_:d_ + 1])
          TS(wk[:, :, :], wk[:, :, :], 0.125, None, ALU.mult, None, ["wk"], ["wk"])
          ACT(G[:, :, :], lg[:, l, :, :], AF.Exp, ["lg"], ["G"], scale=128.0)
          for h in range(4):
              ACT(mtmp[:, 0, :], mconst[:, 0, :], AF.Exp, ["mconst", "lg"], ["mtmp0"], scale=lg[:, l, 0, h:h + 1])
              TT(mtmp[:, 0, :], mtmp[:, 0, :], mconst[:, 2, :], ALU.mult, ["mtmp0", "mconst"], ["mtmp0"])
              ACT(mtmp[:, 1, :], mconst[:, 1, :], AF.Exp, ["mconst", "lg"], ["mtmp1"], scale=lg[:, l, 1, h:h + 1])
              TT(mtmp[:, 1, :], mtmp[:, 1, :], mconst[:, 3, :], ALU.mult, ["mtmp1", "mconst"], ["mtmp1"])
              TT(Mtab[:, h, :], mtmp[:, 0, :], mtmp[:, 1, :], ALU.add, ["mtmp0", "mtmp1"], ["Mtab"])
              for d_ in range(2):
                  ACT(tabQ[:, d_, h, :], efree[:, d_, :], AF.Exp, ["efree", "lg"], ["tabQ"], scale=lg[:, l, d_, h:h + 1])

          m1 = ar.mark()
          krf = [ar.alloc("krf", [128, 512], F32) for _ in range(2)]
          t1f = [ar.alloc("t1f", [128, 512], F32) for _ in range(2)]
          t2f = [ar.alloc("t2f", [128, 512], F32) for _ in range(2)]
          kdt = [ar.alloc("kdt", [128, 256], BF16) for _ in range(2)]
          vbt = [ar.alloc("vbt", [128, 256], BF16) for _ in range(2)]

          def rope_tok(src_ap, src_key, tb, out_ap, out_key, i2, nh):
              W = nh * 64
              if tb < 2:
                  CP(out_ap, src_ap, [src_key], [out_key])
                  return
              n_ = tb - 2
              t1, t2 = t1f[i2], t2f[i2]
              k1, k2 = ("t1f", i2), ("t2f", i2)
              s3 = src_ap.rearrange("p (h d) -> p h d", h=nh)
              s4 = src_ap.rearrange("p (h a d) -> p h a d", h=nh, a=2)
              TT(t1[:, 0:W].rearrange("p (h d) -> p h d", h=nh), s3, bc(cosD[:, n_, :], 1, [128, nh, 64]), ALU.mult,
                 [src_key, "cosD"], [k1])
              t24 = t2[:, 0:W].rearrange("p (h a d) -> p h a d", h=nh, a=2)
              TT(t24[:, :, 0, :], s4[:, :, 1, :], bc(sinS[:, n_, 0:32], 1, [128, nh, 32]), ALU.mult,
                 [src_key, "sinS"], [k2])
              TT(t24[:, :, 1, :], s4[:, :, 0, :], bc(sinS[:, n_, 32:64], 1, [128, nh, 32]), ALU.mult,
                 [src_key, "sinS"], [k2])
              TT(out_ap, t1[:, 0:W], t2[:, 0:W], ALU.add, [k1, k2], [out_key])

          chk(3, l)
          MEMSET(Sst[:, :], 0.0, ["Sst"], eng="dve")
          p2 = {}

          def p2A(it):
              tb = B_ORDER[it]
              i2 = it % 2
              bi = 0 if tb < 2 else 1 + (tb - 2) // 4
              bank = next_bank()
              for k in range(8):
                  MM(bank.ap(0, 128, 0, 512), H[:, k, tb * 128:(tb + 1) * 128], Wret[:, k, 256:768], k == 0, k == 7,
                     [hk(bi), "Wret"], [bank.key])
              rope_tok(bank.ap(0, 128, 0, 256), bank.key, tb, krf[i2][:, 0:256], ("krf", i2), i2, 4)
              TT(kdt[i2][:, :].rearrange("p (h d) -> p h d", h=4), krf[i2][:, 0:256].rearrange("p (h d) -> p h d", h=4),
                 bc(wk[:, 1, :], 2, [128, 4, 64]), ALU.mult, [("krf", i2), "wk"], [("kdt", i2)])
              CP(vbt[i2][:, :], bank.ap(0, 128, 256, 512), [bank.key], [("vbt", i2)], eng="act")

          def p2B(it):
              tb = B_ORDER[it]
              i2 = it % 2
              ub = next_bank()
              for h in range(4):
                  MM(ub.ap(0, 64, h * 64, h * 64 + 64), kdt[i2][:, h * 64:(h + 1) * 64], vbt[i2][:, h * 64:(h + 1) * 64],
                     True, True, [("kdt", i2), ("vbt", i2)], [ub.key])
              CP(snapB[:, tb, :], Sst[:, :], ["Sst"], [("snapB", tb)], eng="act")
              TT(Sst[:, :].rearrange("p (h d) -> p h d", h=4), Sst[:, :].rearrange("p (h d) -> p h d", h=4),
                 bc(G[0:64, 1, :], 2, [64, 4, 64]), ALU.mult, ["Sst", "G"], ["Sst"])
              TT(Sst[:, :], Sst[:, :], ub.ap(0, 64, 0, 256), ALU.add, ["Sst", ub.key], ["Sst"])

          p2A(0)
          for it in range(NT):
              if it + 1 < NT:
                  p2A(it + 1)
              p2B(it)

          chk(4, l)
          qkb = [ar.alloc("qkb", [128, 512], BF16) for _ in range(2)]
          qT = [ar.alloc("qT", [64, 4, 128], BF16) for _ in range(2)]
          kT = [ar.alloc("kT", [64, 4, 128], BF16) for _ in range(2)]
          qfT = [ar.alloc("qfT", [64, 4, 128], BF16) for _ in range(2)]
          qbT = [ar.alloc("qbT", [64, 4, 128], BF16) for _ in range(2)]
          SD = [ar.alloc("SD", [128, 4, 128], BF16) for _ in range(2)]
          st4 = [ar.alloc("st4", [128, 16], F32) for _ in range(2)]
          cen = [ar.alloc("cen", [128, 256], F32) for _ in range(2)]
          gtf = [ar.alloc("gtf", [128, 256], F32) for _ in range(3)]
          rout = [ar.alloc("rout", [128, 256], BF16) for _ in range(2)]
          mixr = [ar.alloc("mixr", [128, 2, 512], BF16) for _ in range(2)]
          MEMSET(Sst[:, :], 0.0, ["Sst"], eng="dve")
          MEMSET(Sbf[:, :], 0.0, ["Sbf"], eng="dve")
          p3 = {}
          p3o = {}

          def p3A(tb):
              i2 = tb % 2
              bi = 0 if tb < 2 else 1 + (tb - 2) // 4
              want_out = bi in qblocks
              qk = Bank(*gen_banks[2 * (tb % 2)])
              vg = Bank(*gen_banks[2 * (tb % 2) + 1])
              p3[tb] = (qk, vg)
              for k in range(8):
                  MM(qk.ap(0, 128, 0, 512), H[:, k, tb * 128:(tb + 1) * 128], Wret[:, k, 0:512], k == 0, k == 7,
                     [hk(bi), "Wret"], [qk.key])
              for k in range(8):
                  MM(vg.ap(0, 128, 0, 512), H[:, k, tb * 128:(tb + 1) * 128], Wret[:, k, 512:1024], k == 0, k == 7,
                     [hk(bi), "Wret"], [vg.key])

          def p3A2(tb):
              i2 = tb % 2
              bi = 0 if tb < 2 else 1 + (tb - 2) // 4
              want_out = bi in qblocks
              qk, vg = p3[tb]
              if want_out:
                  rope_tok(qk.ap(0, 128, 0, 512), qk.key, tb, krf[i2][:, 0:512], ("krf", i2), i2, 8)
              else:
                  rope_tok(qk.ap(0, 128, 256, 512), qk.key, tb, krf[i2][:, 256:512], ("krf", i2), i2, 4)
              TT(kdt[i2][:, :].rearrange("p (h d) -> p h d", h=4), krf[i2][:, 256:512].rearrange("p (h d) -> p h d", h=4),
                 bc(wk[:, 0, :], 2, [128, 4, 64]), ALU.mult, [("krf", i2), "wk"], [("kdt", i2)])
              CP(vbt[i2][:, :], vg.ap(0, 128, 0, 256), [vg.key], [("vbt", i2)], eng="act")
              if want_out:
                  CP(qkb[i2][:, :], krf[i2][:, :], [("krf", i2)], [("qkb", i2)], eng="act")
                  i3 = tb % 3
                  gk = ("gtf", i3)
                  ACT(gtf[i3][:, :], vg.ap(0, 128, 256, 512), AF.Exp, [vg.key], [gk], scale=-1.0)
                  ACT(gtf[i3][:, :], gtf[i3][:, :], AF.Ln, [gk, "epsb"], [gk], bias=epsb[:, 2:3])
                  ACT(gtf[i3][:, :], gtf[i3][:, :], AF.Exp, [gk], [gk], scale=-1.0)
                  TT(gtf[i3][:, :], vg.ap(0, 128, 256, 512), gtf[i3][:, :], ALU.mult, [vg.key, gk], [gk])

          def p3B(tb):
              i2 = tb % 2
              bi = 0 if tb < 2 else 1 + (tb - 2) // 4
              t0b, nb = BLK[bi]
              want_out = bi in qblocks
              if want_out:
                  for h in range(4):
                      TR(psT[0:64, h * 128:(h + 1) * 128], qkb[i2][:, h * 64:(h + 1) * 64], [("qkb", i2)], ["psT"])
                      TR(psT[0:64, (4 + h) * 128:(5 + h) * 128], qkb[i2][:, 256 + h * 64:256 + (h + 1) * 64], [("qkb", i2)], ["psT"])
                  pq = psT[0:64, 0:512].rearrange("p (h i) -> p h i", h=4)
                  pk = psT[0:64, 512:1024].rearrange("p (h i) -> p h i", h=4)
                  CP(qT[i2][:, :, :], pq, ["psT"], [("qT", i2)], eng="act")
                  CP(kT[i2][:, :, :], pk, ["psT"], [("kT", i2)], eng="act")
                  TT(qfT[i2][:, :, :], pq, tabQ[0:64, 0, :, :], ALU.mult, ["psT", "tabQ"], [("qfT", i2)])
                  TT(qbT[i2][:, :, :], pq, tabQ[0:64, 1, :, :], ALU.mult, ["psT", "tabQ"], [("qbT", i2)])
                  stb = Bank(*gen_banks[4])
                  for h in range(4):
                      MM(stb.ap(0, 128, h * 128, (h + 1) * 128), kT[i2][:, h, :], qT[i2][:, h, :], True, True,
                         [("kT", i2), ("qT", i2)], [stb.key])
                  TT(SD[i2][:, :, :], stb.ap(0, 128, 0, 512).rearrange("p (h i) -> p h i", h=4), Mtab[:, :, :], ALU.mult,
                     [stb.key, "Mtab"], [("SD", i2)])

          def p3B2(tb):
              i2 = tb % 2
              bi = 0 if tb < 2 else 1 + (tb - 2) // 4
              t0b, nb = BLK[bi]
              want_out = bi in qblocks
              if want_out:
                  ob = Bank(*gen_banks[5 + (tb % 2)])
                  p3o[tb] = ob
                  for h in range(4):
                      oc_ = ob.ap(0, 128, h * 64, (h + 1) * 64)
                      MM(oc_, qfT[i2][:, h, :], Sbf[:, h * 64:(h + 1) * 64], True, False, [("qfT", i2), "Sbf"], [ob.key])
                      MM(oc_, qbT[i2][:, h, :], snapB[:, tb, h * 64:(h + 1) * 64], False, False,
                         [("qbT", i2), ("snapB", tb)], [ob.key])
                      MM(oc_, SD[i2][:, h, :], vbt[i2][:, h * 64:(h + 1) * 64], False, True,
                         [("SD", i2), ("vbt", i2)], [ob.key])
              ub = Bank(*gen_banks[4])
              for h in range(4):
                  MM(ub.ap(0, 64, h * 64, h * 64 + 64), kdt[i2][:, h * 64:(h + 1) * 64], vbt[i2][:, h * 64:(h + 1) * 64],
                     True, True, [("kdt", i2), ("vbt", i2)], [ub.key])
              TT(Sst[:, :].rearrange("p (h d) -> p h d", h=4), Sst[:, :].rearrange("p (h d) -> p h d", h=4),
                 bc(G[0:64, 0, :], 2, [64, 4, 64]), ALU.mult, ["Sst", "G"], ["Sst"])
              TT(Sst[:, :], Sst[:, :], ub.ap(0, 64, 0, 256), ALU.add, ["Sst", ub.key], ["Sst"])
              CP(Sbf[:, :], Sst[:, :], ["Sst"], ["Sbf"], eng="act")

          def p3C(tb):
              i2 = tb % 2
              bi = 0 if tb < 2 else 1 + (tb - 2) // 4
              t0b, nb = BLK[bi]
              if bi not in qblocks:
                  return
              ob = p3o.pop(tb)
              s4_ = st4[i2]
              sk4 = ("st4", i2)
              o3 = ob.ap(0, 128, 0, 256).rearrange("p (h d) -> p h d", h=4)
              RSUM(s4_[:, 0:4], o3, [ob.key], [sk4])
              TS(s4_[:, 0:4], s4_[:, 0:4], -1.0 / 64.0, None, ALU.mult, None, [sk4], [sk4])
              c3 = cen[i2][:, :].rearrange("p (h d) -> p h d", h=4)
              TT(c3, o3, bc(s4_[:, 0:4], 2, [128, 4, 64]), ALU.add, [ob.key, sk4], [("cen", i2)])
              sq1 = mtmp[:, :, :].rearrange("p a i -> p (a i)")
              TT(sq1, cen[i2][:, :], cen[i2][:, :], ALU.mult, [("cen", i2)], ["sqf1"])
              RSUM(s4_[:, 4:8], sq1.rearrange("p (h d) -> p h d", h=4), ["sqf1"], [sk4])
              ACT(s4_[:, 4:8], s4_[:, 4:8], AF.Ln, [sk4, "epsb"], [sk4], bias=epsb[:, 1:2], scale=1.0 / 64.0)
              ACT(s4_[:, 4:8], s4_[:, 4:8], AF.Exp, [sk4], [sk4], scale=-0.5)
              i3 = tb % 3
              gk = ("gtf", i3)
              TT(c3, c3, bc(s4_[:, 4:8], 2, [128, 4, 64]), ALU.mult, [("cen", i2), sk4], [("cen", i2)])
              TT(rout[i2][:, :], cen[i2][:, :], gtf[i3][:, :], ALU.mult, [("cen", i2), gk], [("rout", i2)])

          def p3Cp(tb):
              i2 = tb % 2
              bi = 0 if tb < 2 else 1 + (tb - 2) // 4
              t0b, nb = BLK[bi]
              if bi not in qblocks:
                  return
              mb = mixr[bi % 2]
              mk_ = ("mixr", bi % 2)
              lt = (tb * 128 - t0b) // 128
              for p in range(2):
                  TR(psT[:, p * 128:(p + 1) * 128], rout[i2][:, p * 128:(p + 1) * 128], [("rout", i2)], ["psT"])
              CP(mb[:, :, lt * 128:(lt + 1) * 128], psT[:, 0:256].rearrange("p (a i) -> p a i", a=2), ["psT"], [mk_], eng="act")
              if tb * 128 + 128 == t0b + nb:
                  for oc in range(8):
                      bank = next_bank([2 * ((tb + 1) % 2), 2 * ((tb + 1) % 2) + 1, 4])
                      for p in range(2):
                          MM(bank.ap(0, 128, 0, nb), Wout_r[:, p, oc * 128:(oc + 1) * 128], mb[:, p, 0:nb], p == 0, p == 1,
                             ["Wout_r", mk_], [bank.key])
                      xupdate(l, 2, oc, bi, t0b, nb, bank)

          p3A(0)
          p3A2(0)
          for tb in range(NT):
              if tb >= 1:
                  p3C(tb - 1)
              if tb + 1 < NT:
                  p3A(tb + 1)
              p3B(tb)
              if tb + 1 < NT:
                  p3A2(tb + 1)
              p3B2(tb)
              if tb >= 1:
                  p3Cp(tb - 1)
          p3C(NT - 1)
          p3Cp(NT - 1)
          ar.release(m0)

          chk(5, l)
          m0 = ar.mark()
          Wu = ar.alloc("Wu", [128, 8, 256], BF16)
          fw.dma("pool", Wu[:, :, :], w_in_d[l].rearrange("(k p) n -> p k n", p=128)[:, :, 1440:1696], writes=["Wu"])
          Wpl = ar.alloc("Wpl", [64, 4, 64], BF16)
          fw.dma("pool", Wpl[:, :, :], w_pool_d[l].rearrange("g c d -> c g d"), writes=["Wpl"])
          Wout_p = ar.alloc("Wout_p", [64, 4, 1024], BF16)
          fw.dma("pool", Wout_p[:, :, :], w_out_d[l, 768:1024, :].rearrange("(g c) n -> c g n", c=64), writes=["Wout_p"])
          bands = ar.alloc("bands", [128, 20, 128], BF16)
          fw.dma("pool", bands[:, :, :], bands_d.rearrange("p (a i) -> p a i", a=20), writes=["bands"])
          utok = ar.alloc("utok", [128, NT, 256], BF16)
          pooledT = [ar.alloc("pooledT", [64, 4, 128], BF16) for _ in range(2)]
          mixp = [ar.alloc("mixp", [64, 4, 512], BF16) for _ in range(2)]
          tiles = list(range(NT)) if l == 0 else list(range(2, NT))
          for tb in tiles:
              bi = 0 if tb < 2 else 1 + (tb - 2) // 4
              bank = next_bank()
              for k in range(8):
                  MM(bank.ap(0, 128, 0, 256), H[:, k, tb * 128:(tb + 1) * 128], Wu[:, k, :], k == 0, k == 7,
                     [hk(bi), "Wu"], [bank.key])
              CP(utok[:, tb, :], bank.ap(0, 128, 0, 256), [bank.key], [("utok", tb)], eng="act")
          for tb in tiles:
              i2 = tb % 2
              bi = 0 if tb < 2 else 1 + (tb - 2) // 4
              t0b, nb = BLK[bi]
              s0, s1 = (0, 1) if tb < 2 else (2, NT - 1)
              first, last = tb == s0, tb == s1
              pb = next_bank()
              for g in range(4):
                  srcs = []
                  if not first:
                      srcs.append((tb - 1, g * 5 + 3))
                  srcs.append((tb, g * 5 + (1 if first else (2 if last else 0))))
                  if not last:
                      srcs.append((tb + 1, g * 5 + 4))
                  for si, (src, bidx) in enumerate(srcs):
                      MM(pb.ap(0, 64, g * 128, (g + 1) * 128), utok[:, src, g * 64:(g + 1) * 64], bands[:, bidx, :],
                         si == 0, si == len(srcs) - 1, [("utok", src), "bands"], [pb.key])
              CP(pooledT[i2][:, :, :], pb.ap(0, 64, 0, 512).rearrange("p (g i) -> p g i", g=4), [pb.key], [("pooledT", i2)],
                 eng="act")
              yb = next_bank()
              for g in range(4):
                  MM(yb.ap(0, 64, g * 128, (g + 1) * 128), Wpl[:, g, :], pooledT[i2][:, g, :], True, True,
                     ["Wpl", ("pooledT", i2)], [yb.key])
              mb = mixp[bi % 2]
              mk_ = ("mixp", bi % 2)
              lt = (tb * 128 - t0b) // 128
              TT(mb[:, :, lt * 128:(lt + 1) * 128], yb.ap(0, 64, 0, 512).rearrange("p (g i) -> p g i", g=4),
                 bc(psc[:, l, :], 2, [64, 4, 128]), ALU.mult, [yb.key, "psc"], [mk_])
              if tb * 128 + 128 == t0b + nb:
                  for oc in range(8):
                      bank = next_bank()
                      for g in range(4):
                          MM(bank.ap(0, 128, 0, nb), Wout_p[:, g, oc * 128:(oc + 1) * 128], mb[:, g, 0:nb], g == 0, g == 3,
                             ["Wout_p", mk_], [bank.key])
                      xupdate(l, 2, oc, bi, t0b, nb, bank)
          ar.release(m0)

          chk(6, l)
          m0 = ar.mark()
          cqn = ar.alloc("cqn", [128, 2, T], BF16)
          ckvn = ar.alloc("ckvn", [128, T], BF16)
          kpe96 = ar.alloc("kpe96", [128, T], BF16)
          ropeM = ar.alloc("ropeM", [128, 2, 2048], F32)
          fw.dma("sp", ropeM[:, :, :], ropeM_d.rearrange("p (a t) -> p a t", a=2), writes=["ropeM"])
          rt1 = [ar.alloc("rt1", [128, 512], F32) for _ in range(2)]
          rt2 = [ar.alloc("rt2", [128, 512], F32) for _ in range(2)]
          m1 = ar.mark()
          Wm1 = ar.alloc("Wm1", [128, 8, 416], BF16)
          Wm2 = ar.alloc("Wm2", [128, 8, 96], BF16)
          fw.dma("pool", Wm1[:, :, :], w_in_d[l].rearrange("(k p) n -> p k n", p=128)[:, :, 1024:1440], writes=["Wm1"])
          fw.dma("pool", Wm2[:, :, :], w_in_d[l].rearrange("(k p) n -> p k n", p=128)[:, :, 1632:1728], writes=["Wm2"])
          sqc = [ar.alloc("sqc", [128, 3, 512], BF16) for _ in range(2)]
          rsc = [ar.alloc("rsc", [128, 2, 512], F32) for _ in range(2)]
          for bi in range(5):
              t0, n = BLK[bi]
              i2 = bi % 2
              need_q = bi in qblocks
              cb = []
              mlist = ([0, 1] if need_q else []) + [2]
              for m in mlist:
                  bank = next_bank()
                  for k in range(8):
                      MM(bank.ap(0, 128, 0, n), Wm1[:, k, m * 128:(m + 1) * 128], H[:, k, t0:t0 + n], k == 0, k == 7,
                         ["Wm1", hk(bi)], [bank.key])
                  ACT(sqc[i2][:, m, 0:n], bank.ap(0, 128, 0, n), AF.Square, [bank.key], [("sqc", i2, m)])
                  cb.append((m, bank))
              if need_q:
                  sb = next_bank()
                  for m in range(2):
                      MM(sb.ap(0, 128, 0, n), ones[:, :], sqc[i2][:, m, 0:n], m == 0, m == 1, ["ones", ("sqc", i2, m)], [sb.key])
                  ACT(rsc[i2][:, 0, 0:n], sb.ap(0, 128, 0, n), AF.Ln, [sb.key, "epsb"], [("rsc", i2, 0)], bias=epsb[:, 1:2],
                      scale=1.0 / 256.0)
                  ACT(rsc[i2][:, 0, 0:n], rsc[i2][:, 0, 0:n], AF.Exp, [("rsc", i2, 0)], [("rsc", i2, 0)], scale=-0.5)
              sb2 = next_bank()
              MM(sb2.ap(0, 128, 0, n), ones[:, :], sqc[i2][:, 2, 0:n], True, True, ["ones", ("sqc", i2, 2)], [sb2.key])
              ACT(rsc[i2][:, 1, 0:n], sb2.ap(0, 128, 0, n), AF.Ln, [sb2.key, "epsb"], [("rsc", i2, 1)], bias=epsb[:, 1:2],
                  scale=1.0 / 128.0)
              ACT(rsc[i2][:, 1, 0:n], rsc[i2][:, 1, 0:n], AF.Exp, [("rsc", i2, 1)], [("rsc", i2, 1)], scale=-0.5)
              for (m, bank) in cb:
                  if m < 2:
                      STT(cqn[:, m, t0:t0 + n], bank.ap(0, 128, 0, n), qn[:, l, m:m + 1], rsc[i2][:, 0, 0:n], ALU.mult, ALU.mult,
                          [bank.key, "qn", ("rsc", i2, 0)], [("cqn", bi)])
                  else:
                      STT(ckvn[:, t0:t0 + n], bank.ap(0, 128, 0, n), kvn[:, l:l + 1], rsc[i2][:, 1, 0:n], ALU.mult, ALU.mult,
                          [bank.key, "kvn", ("rsc", i2, 1)], [("ckvn", bi)])
              kp = next_bank()
              for k in range(8):
                  MM(kp.ap(0, 96, 0, n), Wm1[:, k, 320:416], H[:, k, t0:t0 + n], k == 0, k == 7, ["Wm1", hk(bi)], [kp.key])
              if bi == 0:
                  CP(kpe96[64:96, t0:t0 + n], kp.ap(64, 96, 0, n), [kp.key], [("kpe96", bi)])
              else:
                  ks = next_bank()
                  for k in range(8):
                      MM(ks.ap(0, 96, 0, n), Wm2[:, k, 0:96], H[:, k, t0:t0 + n], k == 0, k == 7, ["Wm2", hk(bi)], [ks.key])
                  lt0 = t0 - 256
                  TT(rt1[i2][64:96, 0:n], kp.ap(64, 96, 0, n), ropeM[64:96, 0, lt0:lt0 + n], ALU.mult, [kp.key, "ropeM"],
                     [("rt1", i2)])
                  TT(rt2[i2][64:96, 0:n], ks.ap(64, 96, 0, n), ropeM[64:96, 1, lt0:lt0 + n], ALU.mult, [ks.key, "ropeM"],
                     [("rt2", i2)])
                  TT(kpe96[64:96, t0:t0 + n], rt1[i2][64:96, 0:n], rt2[i2][64:96, 0:n], ALU.add, [("rt1", i2), ("rt2", i2)],
                     [("kpe96", bi)])
          ar.release(m1)

          chk(7, l)
          Wuq = ar.alloc("Wuq", [128, 2, 1536], BF16)
          fw.dma("pool", Wuq[:, :, :], w_uq_d[l].rearrange("(k p) n -> p k n", p=128), writes=["Wuq"])
          Wukv = ar.alloc("Wukv", [128, 1024], BF16)
          fw.dma("pool", Wukv[:, :], w_ukv_d[l], writes=["Wukv"])
          Wout_m = ar.alloc("Wout_m", [64, 8, 1024], BF16)
          fw.dma("pool", Wout_m[:, :, :], w_out_d[l, 256:768, :].rearrange("(h c) n -> c h n", c=64), writes=["Wout_m"])
          kTm = ar.alloc_at("kTm", [128, 4, T], BF16, H_OFF)
          vaug = ar.alloc_at("vaug", [128, NT, 4, 65], BF16, H_OFF + 4 * T * 2)
          qTh = [ar.alloc("qTh", [128, 512], BF16) for _ in range(8)]
          PT = [ar.alloc("PT", [128, 1024], BF16) for _ in range(3)]
          rden = [ar.alloc("rden", [128, 512], F32) for _ in range(2)]
          bcs = [ar.alloc("bcs", [64, 512], F32) for _ in range(2)]
          mixh = [ar.alloc("mixh", [64, 4, 512], BF16) for _ in range(2)]
          qi = 0
          pti_box = [0]
          oi_box = [0]
          for hh in range(2):
              MEMSET(vaug[:, :, :, 64:65], 1.0, [("vaug", kt) for kt in range(NT)], eng="dve")
              for hl in range(4):
                  h = hh * 4 + hl
                  for bi in range(5):
                      t0, n = BLK[bi]
                      bank = next_bank([0, 1, 2, 3, 6])
                      MM(bank.ap(0, 64, 0, n), Wukv[:, h * 128:h * 128 + 64], ckvn[:, t0:t0 + n], True, True,
                         ["Wukv", ("ckvn", bi)], [bank.key])
                      CP(kTm[0:64, hl, t0:t0 + n], bank.ap(0, 64, 0, n), [bank.key], [("kTm", hl, bi)],
                         eng=("act" if bi % 2 else "dve"))
                      CP(kTm[64:96, hl, t0:t0 + n], kpe96[64:96, t0:t0 + n], [("kpe96", bi)], [("kTm", hl, bi)], eng="dve")
              wv = Wukv[:, :].rearrange("p (h e) -> p h e", h=8)
              for kt in range(NT):
                  bi = 0 if kt < 2 else 1 + (kt - 2) // 4
                  bank = next_bank([0, 1, 2, 3, 6])
                  for hl in range(4):
                      MM(bank.ap(0, 128, hl * 64, hl * 64 + 64), ckvn[:, kt * 128:(kt + 1) * 128],
                         wv[:, hh * 4 + hl, 64:128], True, True, ["Wukv", ("ckvn", bi)], [bank.key])
                  CP(vaug[:, kt, :, 0:64], bank.ap(0, 128, 0, 256).rearrange("p (h e) -> p h e", h=4), [bank.key],
                     [("vaug", kt)], eng=("act" if kt % 2 else "dve"))
              def qproj_items(bi, qset):
                  t0, n = BLK[bi]
                  items = []
                  for hl in range(4):
                      def item(hl=hl):
                          h = hh * 4 + hl
                          q_ = qTh[qset * 4 + hl]
                          qk_ = ("qTh", qset * 4 + hl)
                          qa = next_bank([6, 7])
                          for k in range(2):
                              MM(qa.ap(0, 96, 0, n), Wuq[:, k, h * 192:h * 192 + 96], cqn[:, k, t0:t0 + n], k == 0, k == 1,
                                 ["Wuq", ("cqn", bi)], [qa.key])
                          CP(q_[0:64, 0:n], qa.ap(0, 64, 0, n), [qa.key], [qk_], eng="dve")
                          if bi == 0:
                              CP(q_[64:96, 0:n], qa.ap(64, 96, 0, n), [qa.key], [qk_], eng="dve")
                          else:
                              lt0 = t0 - 256
                              i2 = hl % 2
                              TT(rt1[i2][64:96, 0:n], qa.ap(64, 96, 0, n), ropeM[64:96, 0, lt0:lt0 + n], ALU.mult,
                                 [qa.key, "ropeM"], [("rt1", i2)])
                              qb_ = next_bank([6, 7])
                              for k in range(2):
                                  MM(qb_.ap(0, 96, 0, n), Wuq[:, k, h * 192 + 96:h * 192 + 192], cqn[:, k, t0:t0 + n], k == 0,
                                     k == 1, ["Wuq", ("cqn", bi)], [qb_.key])
                              TT(rt2[i2][64:96, 0:n], qb_.ap(64, 96, 0, n), ropeM[64:96, 1, lt0:lt0 + n], ALU.mult,
                                 [qb_.key, "ropeM"], [("rt2", i2)])
                              TT(q_[64:96, 0:n], rt1[i2][64:96, 0:n], rt2[i2][64:96, 0:n], ALU.add,
                                 [("rt1", i2), ("rt2", i2)], [qk_])
                      items.append(item)
                  return items

              def wout_items(bi, mslot):
                  t0, n = BLK[bi]
                  mb = mixh[mslot]
                  items = []
                  for oc in range(8):
                      def item(oc=oc):
                          bank = next_bank([6, 7])
                          for hl in range(4):
                              MM(bank.ap(0, 128, 0, n), Wout_m[:, hh * 4 + hl, oc * 128:(oc + 1) * 128], mb[:, hl, 0:n],
                                 hl == 0, hl == 3, ["Wout_m", ("mixh", mslot, hl)], [bank.key])
                          xupdate(l, 2, oc, bi, t0, n, bank)
                      items.append(item)
                  return items

              for it_ in qproj_items(qblocks[0], 0):
                  it_()
              prev_w = []
              for bidx, bi in enumerate(qblocks):
                  t0, n = BLK[bi]
                  kts = [0, 1] if bi == 0 else list(range(NT))
                  qset = bidx % 2
                  mslot = bidx % 2
                  mb = mixh[mslot]
                  side = list(prev_w)
                  if bidx + 1 < len(qblocks):
                      side += qproj_items(qblocks[bidx + 1], (bidx + 1) % 2)
                  jobs = []
                  npair = len(kts) // 2
                  for hl in range(4):
                      for pi_ in range(npair):
                          jobs.append(dict(hl=hl, pi=pi_, npair=npair, kts=kts[2 * pi_:2 * pi_ + 2],
                                           q_=qTh[qset * 4 + hl], qk_=("qTh", qset * 4 + hl)))
                  fins = []

                  def emit_S(jb):
                      c = pti_box[0]
                      pti_box[0] += 1
                      sp_ = ps2[c % 2]
                      spk = "b0" if c % 2 == 0 else "b2"
                      spk2 = "b1" if c % 2 == 0 else "b3"
                      pt = PT[c % 3]
                      ptk = ("PT", c % 3)
                      jb["pt"], jb["ptk"] = pt, ptk
                      hl_ = jb["hl"]
                      for j, kt in enumerate(jb["kts"]):
                          kbi = 0 if kt < 2 else 1 + (kt - 2) // 4
                          MM(sp_[:, j * 512:j * 512 + n], kTm[0:96, hl_, kt * 128:(kt + 1) * 128], jb["q_"][0:96, 0:n],
                             True, True, [("kTm", hl_, kbi), jb["qk_"]], [spk, spk2])
                      ACT(pt[:, :].rearrange("p (a i) -> p a i", a=2)[:, :, 0:n],
                          sp_[:, :].rearrange("p (a i) -> p a i", a=2)[:, :, 0:n], AF.Exp, [spk, spk2], [ptk], scale=SCALE_MLA)

                  fin_a_done = set()

                  def emit_fin_a(hl_, opar):
                      okey = "b4" if opar == 0 else "b5"
                      obank = ps1[opar]
                      rd, rdk = rden[opar], ("rden", opar)
                      ACT(rd[64:65, 0:n], obank[64:65, 0:n], AF.Ln, [okey], [rdk])
                      ACT(rd[64:65, 0:n], rd[64:65, 0:n], AF.Exp, [rdk], [rdk], scale=-1.0)
                      fin_a_done.add((hl_, opar))

                  def emit_fin(hl_, opar):
                      if (hl_, opar) not in fin_a_done:
                          emit_fin_a(hl_, opar)
                      okey = "b4" if opar == 0 else "b5"
                      obank = ps1[opar]
                      rd, rdk = rden[opar], ("rden", opar)
                      bb = next_bank([6, 7])
                      MM(bb.ap(0, 64, 0, n), onesf[64:65, 0:64], rd[64:65, 0:n], True, True, [rdk, "onesf"], [bb.key])
                      CP(bcs[opar][:, 0:n], bb.ap(0, 64, 0, n), [bb.key], [("bcs", opar)], eng="dve")
                      TT(mb[:, hl_, 0:n], obank[0:64, 0:n], bcs[opar][:, 0:n], ALU.mult, [okey, ("bcs", opar)],
                         [("mixh", mslot, hl_)])

                  def emit_PV(jb):
                      hl_, pi_ = jb["hl"], jb["pi"]
                      if pi_ == 0:
                          oi_box[0] += 1
                      opar = oi_box[0] % 2
                      if pi_ == 0:
                          for f_ in list(fins):
                              if f_[0][1] == opar:
                                  emit_fin(*f_[0])
                                  fins.remove(f_)
                      okey = "b4" if opar == 0 else "b5"
                      obank = ps1[opar]
                      pt, ptk = jb["pt"], jb["ptk"]
                      for j, kt in enumerate(jb["kts"]):
                          MM(obank[0:65, 0:n], vaug[:, kt, hl_, :], pt[:, j * 512:j * 512 + n],
                             (pi_ == 0 and j == 0), (pi_ == jb["npair"] - 1 and j == 1), [ptk, ("vaug", kt)], [okey])
                      if pi_ == jb["npair"] - 1:
                          return (hl_, opar)
                      return None

                  pend = []
                  for ji, jb in enumerate(jobs):
                      emit_S(jb)
                      pend.append(jb)
                      if len(pend) > 2:
                          r_ = emit_PV(pend.pop(0))
                          if r_ is not None:
                              fins.append([r_, 6])
                      for f_ in list(fins):
                          f_[1] -= 1
                          if f_[1] == 3 and f_[0] not in fin_a_done:
                              emit_fin_a(*f_[0])
                          if f_[1] <= 0:
                              emit_fin(*f_[0])
                              fins.remove(f_)
                      if side and ji % 3 == 2:
                          side.pop(0)()
                  while pend:
                      r_ = emit_PV(pend.pop(0))
                      if r_ is not None:
                          fins.append([r_, 0])
                  for f_ in fins:
                      emit_fin(*f_[0])
                  while side:
                      side.pop(0)()
                  prev_w = wout_items(bi, mslot)
              for it_ in prev_w:
                  it_()
          ar.release(m0)

          chk(8, l)
          m0 = ar.mark()
          chk(8, l)
          NSLOT = 3
          W1 = [ar.alloc("W1", [128, 8, 512], BF16) for _ in range(NSLOT)]
          W2 = [ar.alloc("W2", [128, 4, 1024], BF16) for _ in range(NSLOT)]
          rl = [ar.alloc("rl", [128, 512], F32) for _ in range(2)]
          h1 = [ar.alloc("h1", [128, 4, 512], BF16) for _ in range(2)]
          NORM_KEYS = [("sq", 0), ("sq", 1), ("rs", 0), ("rs", 1), ("tmpf", 0), ("tmpf", 1)]
          if l == 0:
              o0 = ar.off
              wada1 = [ar.alloc("wada", [128, 8, 1024], BF16) for _ in range(2)]
              o1 = o0 + 16384
              sq = [ar.alloc_at("sq", [128, 8, 512], BF16, o0 + 8192 * i) for i in range(2)]
              rs = [ar.alloc_at("rs", [128, 512], F32, o1 + 2048 * i) for i in range(2)]
              tmpf = [ar.alloc_at("tmpf", [128, 512], F32, o1 + 4096 + 2048 * i) for i in range(2)]
          else:
              sq = [ar.alloc("sq", [128, 8, 512], BF16) for _ in range(2)]
              rs = [ar.alloc("rs", [128, 512], F32) for _ in range(2)]
              tmpf = [ar.alloc("tmpf", [128, 512], F32) for _ in range(2)]
              ost1 = [ar.alloc("ost", [128, 512], F32) for _ in range(4)]
          FFN_BANKS = [0, 1, 2, 3, 4, 5] if l == 0 else [0, 1, 2, 3, 4, 5, 6]
          it = 0
          for g in range(8):
              if l == 0 and 1 <= g <= 6:
                  adaln_piece(1, g - 1, wada1, 6, alias=NORM_KEYS)
                  if g == 6:
                      adaln_A(1, 0)
                      adaln_A(1, 1)
              s = g % NSLOT
              fw.dma("pool", W1[s][:, :, :], w_ff1_d[l].rearrange("(k p) n -> p k n", p=128)[:, :, g * 512:(g + 1) * 512],
                     writes=[("W1", s)])
              fw.dma("pool", W2[s][:, :, :], w_ff2_d[l, g * 512:(g + 1) * 512, :].rearrange("(k p) n -> p k n", p=128),
                     writes=[("W2", s)])
              for bi in qblocks:
                  t0, n = BLK[bi]
                  if g == 0:
                      def _n2(b_):
                          norm_pass(l, lambda c, w: A2[:, l, c, w:w + 1], lambda c, w: mod(l, 3, c, w), [b_],
                                    lambda c, bi_, t0_, n_: (H[:, c, t0_:t0_ + n_], hk(bi_), (lambda c_, w_: mod(l, 3, c_, w_))),
                                    (sq, rs, tmpf), [("A", l, 1), ("modT", l, 3)])
                      qi_ = qblocks.index(bi)
                      if qi_ == 0:
                          _n2(bi)
                      if qi_ + 1 < len(qblocks):
                          _n2(qblocks[qi_ + 1])
                      chk(9, l)
                  hb = h1[it % 2]
                  hbk = ("h1", it % 2)
                  it += 1
                  for f in range(4):
                      bank = next_bank(FFN_BANKS)
                      for k in range(8):
                          MM(bank.ap(0, 128, 0, n), W1[s][:, k, f * 128:(f + 1) * 128], H[:, k, t0:t0 + n], k == 0, k == 7,
                             [("W1", s), hk(bi)], [bank.key])
                      r_ = rl[f % 2]
                      ACT(r_[:, 0:n], bank.ap(0, 128, 0, n), AF.Relu, [bank.key], [("rl", f % 2)])
                      TT(hb[:, f, 0:n], r_[:, 0:n], r_[:, 0:n], ALU.mult, [("rl", f % 2)], [(hbk, f)])
                  for oc in range(8):
                      bank = next_bank(FFN_BANKS)
                      for f in range(4):
                          MM(bank.ap(0, 128, 0, n), W2[s][:, f, oc * 128:(oc + 1) * 128], hb[:, f, 0:n], f == 0, f == 3,
                             [("W2", s), (hbk, f)], [bank.key])
                      xupdate(l, 5, oc, bi, t0, n, bank)
                  if l == 0 and g == 7:
                      def _n1(b_):
                          norm_pass(1, lambda c, w: A1[:, 1, c, w:w + 1], lambda c, w: mod(1, 0, c, w), [b_],
                                    lambda c, bi_, t0_, n_: (H[:, c, t0_:t0_ + n_], hk(bi_), (lambda c_, w_: mod(1, 0, c_, w_))),
                                    (sq, rs, tmpf), [("A", 1, 0), ("modT", 1, 0)])
                      qj_ = qblocks.index(bi)
                      if qj_ >= 1:
                          _n1(qblocks[qj_ - 1])
                      if qj_ == len(qblocks) - 1:
                          _n1(bi)
                  if l == 1 and g == 7:
                      qi_ = qblocks.index(bi)
                      if qi_ >= 1:
                          final_norm_block(qblocks[qi_ - 1], sq, rs, ost1)
                      if qi_ == len(qblocks) - 1:
                          final_norm_block(bi, sq, rs, ost1)
                      fin_state["done"] = True
          ar.release(m0)


    except _Stop:
        fw.barrier()
        ar.off = persist_mark

    if not fin_state["done"]:
        if stage < 99:
            ar.off = 16512 + 180000
        sq = [ar.alloc("sq", [128, 8, 512], BF16) for _ in range(2)]
        rs = [ar.alloc("rs", [128, 512], F32) for _ in range(2)]
        ost = [ar.alloc("ost", [128, 512], F32) for _ in range(4)]
        for bi in [1, 2, 3, 4]:
            final_norm_block(bi, sq, rs, ost)
    outs = fin_state["outs"]
    fw.finish("sp", outs)
    fw.emit()
    return nc


def _consts():
    L, GW = 2048, 64
    t = np.arange(L)
    row = (t // GW).astype(np.float32)
    col = (t % GW).astype(np.float32)

    def tables(dim):
        nf = dim // 4
        inv = (10000.0 ** (-np.arange(nf, dtype=np.float32) / nf)).astype(np.float32)
        ang = np.concatenate([row[:, None] * inv, col[:, None] * inv], axis=-1).astype(np.float32)
        return np.cos(ang).astype(np.float32), np.sin(ang).astype(np.float32)

    cr, sr = tables(64)
    cosD = np.concatenate([cr, cr], axis=-1).reshape(16, 128, 64).transpose(1, 0, 2).reshape(128, 16 * 64)
    sinS = np.concatenate([-sr, sr], axis=-1).reshape(16, 128, 64).transpose(1, 0, 2).reshape(128, 16 * 64)
    cm, sm = tables(32)
    ropeM = np.zeros((128, 2, L), np.float32)
    ropeM[64:96, 0, :] = np.concatenate([cm, cm], axis=-1).T
    ropeM[64:96, 1, :] = np.concatenate([-sm, sm], axis=-1).T
    bands = np.zeros((20, 128, 128), np.float32)
    for wi, w in enumerate((2, 4, 8, 16)):
        def mat(Lseq, tile_lo, src_lo):
            m = np.zeros((128, 128), np.float32)
            for tt in range(128):
                tg = tile_lo + tt
                lo = min(max(tg - w // 2, 0), Lseq)
                hi = min(max(tg - w // 2 + w, 0), Lseq)
                for sg in range(lo, hi):
                    sl = sg - src_lo
                    if 0 <= sl < 128:
                        m[sl, tt] += 1.0 / (hi - lo)
                sl = tg - src_lo
                if 0 <= sl < 128:
                    m[sl, tt] -= 1.0
            return m
        Ls = 1024
        bands[wi * 5 + 0] = mat(Ls, 256, 256)
        bands[wi * 5 + 1] = mat(Ls, 0, 0)
        bands[wi * 5 + 2] = mat(Ls, Ls - 128, Ls - 128)
        bands[wi * 5 + 3] = mat(Ls, 256, 128)
        bands[wi * 5 + 4] = mat(Ls, 256, 384)
    bands = bands.transpose(1, 0, 2).reshape(128, 20 * 128)
    j = np.arange(128, dtype=np.float32)[:, None]
    i = np.arange(128, dtype=np.float32)[None, :]
    mconst = np.stack([np.maximum(i - j, 0), np.maximum(j - i, 0), (i >= j) * 0.125, (j > i) * 0.125], axis=1)
    mconst = mconst.astype(np.float32).reshape(128, 4 * 128)
    efree = np.stack([np.broadcast_to(i + 1.0, (128, 128)), np.broadcast_to(128.0 - i, (128, 128))], axis=1)
    efree = np.ascontiguousarray(efree, dtype=np.float32).reshape(128, 2 * 128)
    ecol = np.concatenate([127.0 - j, j], axis=1).astype(np.float32)
    return dict(cosD=np.ascontiguousarray(cosD), sinS=np.ascontiguousarray(sinS), ropeM=ropeM.reshape(128, 2 * L),
                bands=np.ascontiguousarray(bands), mconst=mconst, efree=efree, ecol=ecol)


def _fm(v, lead):
    v = np.asarray(v, np.float32)
    a = v.reshape(lead, -1, 128)
    return np.ascontiguousarray(a.transpose(2, 0, 1).reshape(128, -1))


_NC_CACHE = {}


def kernel(x, c, ctx, c_ctx, w_ada, b_ada, norm_mix, w_in, q_norm, w_uq, kv_norm, w_ukv,
           ret_decay_logit, w_pool, pool_scale, w_out, norm_mlp, w_ff1, w_ff2, norm_final):
    f = lambda a: np.ascontiguousarray(np.asarray(a, dtype=np.float32))
    x, c, ctx, c_ctx = f(x), f(c), f(ctx), f(c_ctx)
    w_in = f(w_in)
    kpe = w_in[:, :, 1408:1440]
    w_in_ext = np.concatenate([w_in, kpe[:, :, 16:32], kpe[:, :, 0:16]], axis=2)
    wq = f(w_uq).reshape(2, 256, 8, 96)
    nope, rp = wq[..., 0:64], wq[..., 64:96]
    rp_sw = np.concatenate([rp[..., 16:32], rp[..., 0:16]], axis=-1)
    w_uq_ext = np.ascontiguousarray(np.concatenate([nope, rp, nope, rp_sw], axis=-1).reshape(2, 256, 1536))
    shared = dict(
        w_ada=f(w_ada), b_adaT=_fm(f(b_ada).reshape(2, 6144), 2), gmix=_fm(norm_mix, 2), gmlp=_fm(norm_mlp, 2),
        gfin=_fm(f(norm_final)[None], 1), w_in_ext=np.ascontiguousarray(w_in_ext), qnorm=_fm(q_norm, 2),
        kvnorm=_fm(kv_norm, 2), w_uq_ext=w_uq_ext, w_ukv=f(w_ukv),
        logit=np.ascontiguousarray(np.broadcast_to(f(ret_decay_logit).reshape(1, 16), (128, 16))),
        w_pool=f(w_pool), pscale=np.ascontiguousarray(f(pool_scale).reshape(2, 4, 64).transpose(2, 0, 1).reshape(64, 8)),
        w_out=f(w_out), w_ff1=f(w_ff1), w_ff2=f(w_ff2))
    shared.update(_consts())
    in_maps = []
    for b in range(8):
        xT = np.ascontiguousarray(np.concatenate([ctx[b].T, x[b].T], axis=1))
        cc = np.stack([c[b].reshape(8, 128).T, c_ctx.reshape(8, 128).T], axis=2).reshape(128, 16)
        m = dict(shared)
        m["xT"] = xT
        m["cc"] = np.ascontiguousarray(cc)
        in_maps.append(m)
    if _NC_CACHE.get("prep_only"):
        return in_maps
    if "nc" not in _NC_CACHE:
        _NC_CACHE["nc"] = build_nc()
    used = _NC_CACHE["nc"]._used_inputs
    in_maps = [{k: v for k, v in m.items() if k in used} for m in in_maps]
    res = run_bass_kernel_spmd(_NC_CACHE["nc"], in_maps, core_ids=list(range(8)))
    out = np.stack([np.ascontiguousarray(r["outT"].T) for r in res.results], axis=0)
    return out.astype(np.float32)
```

```python
import numpy as np
from contextlib import ExitStack
import concourse.bass as bass
import concourse.mybir as mybir
from concourse.bass_utils import run_bass_kernel_spmd

F32 = mybir.dt.float32
BF16 = mybir.dt.bfloat16
ALU = mybir.AluOpType
AF = mybir.ActivationFunctionType
AX = mybir.AxisListType

ENGS = ("pe", "act", "dve", "pool", "sp")
D = 1024
T = 2304
NT = 18
BLK = [(0, 256), (256, 512), (768, 512), (1280, 512), (1792, 512)]
EPS = 1e-6
SCALE_MLA = 96.0 ** -0.5


class _Op:
    __slots__ = ("fn", "waits", "milestone", "seen", "dma_sem")

    def __init__(self, fn, waits, seen, dma_sem=None):
        self.fn = fn
        self.waits = waits
        self.milestone = False
        self.seen = seen
        self.dma_sem = dma_sem


class _Region:
    __slots__ = ("writer", "readers")

    def __init__(self):
        self.writer = None
        self.readers = {}


class FW:
    N_DMA_SEMS = 24

    def __init__(self, nc):
        self.nc = nc
        self.ops = {e: [] for e in ENGS}
        self.seen = {e: {} for e in ENGS}
        self.regs = {}
        self.dma_cnt = {}
        self.dma_rr = {"sp": 0, "pool": 0}

    def region(self, key):
        r = self.regs.get(key)
        if r is None:
            r = self.regs[key] = _Region()
        return r

    def _record(self, eng, fn, reads, writes, extra=None, dma_sem=None):
        deps = dict(extra) if extra else {}

        def add(k, v):
            if deps.get(k, -1) < v:
                deps[k] = v

        for key in reads:
            R = self.region(key)
            if R.writer is not None:
                add(*R.writer)
            if isinstance(key, str) and (key == "psT" or (key[0] == "b" and key[1:].isdigit())):
                for k, v in R.readers.items():
                    if k != eng:
                        add(k, v)
        for key in writes:
            R = self.region(key)
            if R.writer is not None:
                add(*R.writer)
            for k, v in R.readers.items():
                add(k, v)
        idx = len(self.ops[eng])
        seen = self.seen[eng]
        waits = {}
        for k, v in deps.items():
            if k == eng and eng in ("pe", "sp"):
                continue
            if seen.get(k, -1) >= v:
                continue
            waits[k] = v
            seen[k] = v
            if k in ENGS:
                tgt = self.ops[k][v]
                tgt.milestone = True
                for e2, v2 in zip(ENGS, tgt.seen):
                    if seen.get(e2, -1) < v2:
                        seen[e2] = v2
        snap = tuple(idx if e == eng else seen.get(e, -1) for e in ENGS)
        self.ops[eng].append(_Op(fn, waits, snap, dma_sem))
        return idx

    def op(self, eng, fn, reads=(), writes=()):
        idx = self._record(eng, fn, reads, writes)
        for key in reads:
            self.region(key).readers[eng] = idx
        for key in writes:
            R = self.region(key)
            R.writer = (eng, idx)
            R.readers = {}
        return idx

    def dma(self, queue, out, in_, reads=(), writes=()):
        i = self.dma_rr[queue]
        self.dma_rr[queue] = (i + 1) % self.N_DMA_SEMS
        skey = ("d", queue, i)
        prev = self.dma_cnt.get(skey, 0)
        val = prev + 16
        self.dma_cnt[skey] = val
        extra = {skey: prev} if prev > 0 else None

        def fn(e, out=out, in_=in_):
            return e.dma_start(out=out, in_=in_)

        self._record(queue, fn, reads, writes, extra=extra, dma_sem=skey)
        for key in reads:
            self.region(key).readers[skey] = val
        for key in writes:
            R = self.region(key)
            R.writer = (skey, val)
            R.readers = {}

    def barrier(self):
        deps = {}
        for e in ("pe", "act", "dve", "pool"):
            lst = self.ops[e]
            for i in range(len(lst) - 1, -1, -1):
                if lst[i].fn is not None and lst[i].dma_sem is None:
                    deps[e] = i
                    break
        for k, v in self.dma_cnt.items():
            deps[k] = v
        for e in ENGS:
            ex = {k: v for k, v in deps.items() if k != e}
            self._record(e, None, (), (), extra=ex)

    def finish(self, eng, keys):
        self._record(eng, None, keys, ())

    def emit(self):
        nc = self.nc
        counts = {}
        for e in ENGS:
            c = 0
            lst = []
            for o in self.ops[e]:
                if o.milestone:
                    assert o.fn is not None and o.dma_sem is None
                    c += 1
                lst.append(c)
            counts[e] = lst
        with ExitStack() as st:
            sems = {e: st.enter_context(nc.semaphore("s_" + e)) for e in ENGS}
            for k in self.dma_cnt:
                sems[k] = st.enter_context(nc.semaphore("d_%s_%d" % (k[1], k[2])))
            block = st.enter_context(nc.Block())

            def run(ename, e):
                for o in self.ops[ename]:
                    for k, v in o.waits.items():
                        if k in ENGS:
                            e.wait_ge(sems[k], counts[k][v])
                        else:
                            e.wait_ge(sems[k], v)
                    if o.fn is None:
                        continue
                    ins = o.fn(e)
                    if o.dma_sem is not None:
                        ins.then_inc(sems[o.dma_sem], 16)
                    elif o.milestone:
                        ins.then_inc(sems[ename], 1)

            @block.tensor
            def _(e):
                run("pe", e)

            @block.scalar
            def _(e):
                run("act", e)

            @block.vector
            def _(e):
                run("dve", e)

            @block.gpsimd
            def _(e):
                run("pool", e)

            @block.sync
            def _(e):
                run("sp", e)


DBG = {}


class Arena:
    def __init__(self, nc, fw, base, limit):
        self.nc, self.fw, self.off, self.limit = nc, fw, base, limit
        self.n = 0

    def alloc(self, name, shape, dt):
        esz = 2 if dt == BF16 else 4
        sz = esz
        for s in shape[1:]:
            sz *= s
        sz = (sz + 63) // 64 * 64
        off = self.off
        assert off + sz <= self.limit, "SBUF arena overflow: %s needs %d at %d (limit %d)" % (name, sz, off, self.limit)
        self.off += sz
        self.n += 1
        t = self.nc.alloc_sbuf_tensor_at("%s_%d" % (name, self.n), list(shape), dt, offset=off)
        DBG[name] = t
        return t

    def alloc_at(self, name, shape, dt, off):
        self.n += 1
        t = self.nc.alloc_sbuf_tensor_at("%s_%d" % (name, self.n), list(shape), dt, offset=off)
        DBG[name] = t
        return t

    def mark(self):
        return self.off

    def release(self, m):
        self.fw.barrier()
        self.off = m


class _Stop(Exception):
    pass


def build_nc(stage=None):
    import os
    if stage is None:
        stage = int(os.environ.get('KSTAGE', '99'))
    nc = bass.Bass("TRN2", target_bir_lowering=False)
    fw = FW(nc)

    used_inputs = set()
    nc._used_inputs = used_inputs

    class LazyAP:
        def __init__(self, name, shape):
            self.name, self.shape_, self._t = name, list(shape), None

        def ap(self):
            if self._t is None:
                self._t = nc.dram_tensor(self.name, self.shape_, F32, kind="ExternalInput").ap()
                used_inputs.add(self.name)
            return self._t

        def __getitem__(self, k):
            return self.ap()[k]

        def rearrange(self, *a, **k):
            return self.ap().rearrange(*a, **k)

    def din(name, shape):
        return LazyAP(name, shape)

    _odma = fw.dma

    def _dma(queue, out, in_, reads=(), writes=()):
        if isinstance(in_, LazyAP):
            in_ = in_.ap()
        return _odma(queue, out, in_, reads=reads, writes=writes)

    fw.dma = _dma

    xT_d = din("xT", [D, T])
    cc_d = din("cc", [128, 16])
    w_ada_d = din("w_ada", [2, D, 6144])
    b_ada_d = din("b_adaT", [128, 96])
    gmix_d = din("gmix", [128, 16])
    gmlp_d = din("gmlp", [128, 16])
    gfin_d = din("gfin", [128, 8])
    w_in_d = din("w_in_ext", [2, D, 1728])
    qn_d = din("qnorm", [128, 4])
    kvn_d = din("kvnorm", [128, 2])
    w_uq_d = din("w_uq_ext", [2, 256, 1536])
    w_ukv_d = din("w_ukv", [2, 128, 1024])
    logit_d = din("logit", [128, 16])
    w_pool_d = din("w_pool", [2, 4, 64, 64])
    psc_d = din("pscale", [64, 8])
    w_out_d = din("w_out", [2, D, D])
    w_ff1_d = din("w_ff1", [2, D, 4096])
    w_ff2_d = din("w_ff2", [2, 4096, D])
    cosD_d = din("cosD", [128, 16 * 64])
    sinS_d = din("sinS", [128, 16 * 64])
    ropeM_d = din("ropeM", [128, 2 * 2048])
    bands_d = din("bands", [128, 20 * 128])
    mconst_d = din("mconst", [128, 4 * 128])
    efree_d = din("efree", [128, 2 * 128])
    ecol_d = din("ecol", [128, 2])
    outT_d = nc.dram_tensor("outT", [D, 2048], F32, kind="ExternalOutput").ap()

    ps2 = [nc.alloc_psum_tensor("ps2_%d" % i, [128, 1024], F32) for i in range(2)]
    ps1 = [nc.alloc_psum_tensor("ps1_%d" % i, [128, 512], F32) for i in range(3)]
    psT = nc.alloc_psum_tensor("psT", [128, 1024], BF16)
    gen_banks = [(ps2[0], 0, "b0"), (ps2[0], 512, "b1"), (ps2[1], 0, "b2"), (ps2[1], 512, "b3"),
                 (ps1[0], 0, "b4"), (ps1[1], 0, "b5"), (ps1[2], 0, "b6")]
    rr = {"i": 0}

    class Bank:
        def __init__(self, t, off, key):
            self.t, self.off, self.key = t, off, key

        def ap(self, p0, p1, c0, c1):
            return self.t[p0:p1, self.off + c0:self.off + c1]

    psT_f32 = psT[:, :].bitcast(F32)

    class BankAP:
        def __init__(self, ap2, key):
            self.a, self.key = ap2, key

        def ap(self, p0, p1, c0, c1):
            return self.a[p0:p1, c0:c1]

    def next_bank(sel=None):
        lst = gen_banks if sel is None else [gen_banks[i] if i < 7 else None for i in sel]
        b = lst[rr["i"] % len(lst)]
        rr["i"] += 1
        if b is None:
            return BankAP(psT_f32, "psT")
        return Bank(*b)

    TOTAL = 212000
    ar = Arena(nc, fw, 16512, 229344)
    X = ar.alloc("X", [128, 8, T], F32)
    H_OFF = ar.off
    H = ar.alloc("H", [128, 8, T], BF16)
    ident = ar.alloc("ident", [128, 128], BF16)
    ones = ar.alloc("ones", [128, 128], BF16)
    onesf = ar.alloc("onesf", [128, 64], F32)
    cc = ar.alloc("cc", [128, 8, 2], F32)
    silc = ar.alloc("silc", [128, 8, 2], BF16)
    bT = ar.alloc("bT", [128, 2, 48], F32)
    modT = ar.alloc("modT", [128, 2, 48, 2], F32)
    gmix = ar.alloc("gmix", [128, 2, 8], F32)
    gmlp = ar.alloc("gmlp", [128, 2, 8], F32)
    gfin = ar.alloc("gfin", [128, 8], F32)
    A1 = ar.alloc("A1", [128, 2, 8, 2], F32)
    A2 = ar.alloc("A2", [128, 2, 8, 2], F32)
    AF_ = ar.alloc("AFin", [128, 8], F32)
    qn = ar.alloc("qn", [128, 2, 2], F32)
    kvn = ar.alloc("kvn", [128, 2], F32)
    psc = ar.alloc("psc", [64, 2, 4], F32)
    lgt = ar.alloc("lgt", [128, 16], F32)
    lg = ar.alloc("lg", [128, 2, 2, 4], F32)
    ecol = ar.alloc("ecol", [128, 2], F32)
    small = ar.alloc("small", [128, 64], F32)
    epsb = ar.alloc("epsb", [128, 4], F32)

    def xk(c, bi):
        return ("X", c, bi)

    def hk(bi):
        return ("H", bi)

    def MM(out, lhsT, rhs, start, stop, r, w):
        fw.op("pe", lambda e: e.matmul(out, lhsT, rhs, start=start, stop=stop), r, w)

    def TR(out, in_, r, w):
        p = in_.shape[0]
        fw.op("pe", lambda e: e.transpose(out, in_, ident[0:p, 0:p]), list(r) + ["ident"], w)

    def ACT(out, in_, func, r, w, bias=None, scale=None):
        kw = {}
        if bias is not None:
            kw["bias"] = bias
        if scale is not None:
            kw["scale"] = scale
        fw.op("act", lambda e: e.activation(out=out, in_=in_, func=func, **kw), r, w)

    def TT(out, in0, in1, op, r, w, eng="dve"):
        fw.op(eng, lambda e: e.tensor_tensor(out=out, in0=in0, in1=in1, op=op), r, w)

    def TS(out, in0, s1, s2, op0, op1, r, w, eng="dve"):
        if op1 is None:
            fw.op(eng, lambda e: e.tensor_scalar(out=out, in0=in0, scalar1=s1, scalar2=None, op0=op0), r, w)
        else:
            fw.op(eng, lambda e: e.tensor_scalar(out=out, in0=in0, scalar1=s1, scalar2=s2, op0=op0, op1=op1), r, w)

    def STT(out, in0, scalar, in1, op0, op1, r, w, eng="dve"):
        fw.op(eng, lambda e: e.scalar_tensor_tensor(out=out, in0=in0, scalar=scalar, in1=in1, op0=op0, op1=op1), r, w)

    def CP(out, in_, r, w, eng="dve"):
        if eng == "act":
            ACT(out, in_, AF.Copy, r, w)
        else:
            fw.op(eng, lambda e: e.tensor_copy(out=out, in_=in_), r, w)

    def RSUM(out, in_, r, w):
        fw.op("dve", lambda e: e.reduce_sum(out=out, in_=in_, axis=AX.X), r, w)

    def RECIP(out, in_, r, w):
        fw.op("dve", lambda e: e.reciprocal(out=out, in_=in_), r, w)

    def MEMSET(ap, val, w, eng="pool"):
        fw.op(eng, lambda e: e.memset(ap, val), (), w)

    def bc(ap, axis, shape):
        return ap.unsqueeze(axis).broadcast_to(list(shape))

    for c in range(8):
        fw.dma("sp", X[:, c, :], xT_d[c * 128:(c + 1) * 128, :], writes=[xk(c, bi) for bi in range(5)])
    fw.dma("sp", cc[:, :, :], cc_d.rearrange("p (k w) -> p k w", w=2), writes=["cc"])
    fw.dma("sp", bT[:, :, :], b_ada_d.rearrange("p (l m) -> p l m", l=2), writes=["bT"])
    fw.dma("sp", gmix[:, :, :], gmix_d.rearrange("p (l c) -> p l c", l=2), writes=["gmix"])
    fw.dma("sp", gmlp[:, :, :], gmlp_d.rearrange("p (l c) -> p l c", l=2), writes=["gmlp"])
    fw.dma("sp", gfin[:, :], gfin_d, writes=["gfin"])
    fw.dma("sp", qn[:, :, :], qn_d.rearrange("p (l c) -> p l c", l=2), writes=["qn"])
    fw.dma("sp", kvn[:, :], kvn_d, writes=["kvn"])
    fw.dma("sp", psc[:, :, :], psc_d.rearrange("p (l g) -> p l g", l=2), writes=["psc"])
    fw.dma("sp", lgt[:, :], logit_d, writes=["lgt"])
    fw.dma("sp", ecol[:, :], ecol_d, writes=["ecol"])
    MEMSET(ident[:, :], 0.0, ["ident"])
    fw.op("pool", lambda e: e.affine_select(out=ident[:, :], in_=ident[:, :], pattern=[[-1, 128]],
                                            compare_op=ALU.not_equal, fill=1.0, base=0, channel_multiplier=1),
          ["ident"], ["ident"])
    MEMSET(ones[:, :], 1.0, ["ones"])
    MEMSET(onesf[:, :], 1.0, ["onesf"])
    MEMSET(epsb[:, 0:1], 1024.0 * EPS, ["epsb"])
    MEMSET(epsb[:, 1:2], EPS, ["epsb"])
    MEMSET(epsb[:, 2:3], 1.0, ["epsb"])
    MEMSET(epsb[:, 3:4], 0.0, ["epsb"])

    ccf = cc[:, :, :].rearrange("p k w -> p (k w)")
    ACT(small[:, 0:16], ccf, AF.Exp, ["cc"], ["small"], scale=-1.0)
    TS(small[:, 0:16], small[:, 0:16], 1.0, None, ALU.add, None, ["small"], ["small"])
    RECIP(small[:, 0:16], small[:, 0:16], ["small"], ["small"])
    TT(silc[:, :, :].rearrange("p k w -> p (k w)"), ccf, small[:, 0:16], ALU.mult, ["cc", "small"], ["silc"])
    lgf = lg[:, :, :, :].rearrange("p l d h -> p (l d h)")
    ACT(small[:, 16:32], lgt[:, :], AF.Exp, ["lgt"], ["small2"], scale=-1.0)
    ACT(small[:, 16:32], small[:, 16:32], AF.Ln, ["small2", "epsb"], ["small2"], bias=epsb[:, 2:3])
    TS(lgf, small[:, 16:32], -1.0, None, ALU.mult, None, ["small2"], ["lg"])

    TS(AF_[:, :], gfin[:, :], 32.0, None, ALU.mult, None, ["gfin"], ["AFin"])
    persist_mark = ar.mark()
    ada_cnt = {"i": 0}

    def adaln_piece(l, j, wada, bankidx, alias=()):
        i = ada_cnt["i"]
        ada_cnt["i"] += 1
        slot = wada[i % 2]
        sk = ("wada", i % 2)
        bank = next_bank([bankidx])
        fw.dma("pool", slot[:, :, :],
               w_ada_d[l].rearrange("(k p) n -> p k n", p=128)[:, :, j * 1024:(j + 1) * 1024], writes=[sk] + list(alias))
        for m in range(8):
            for k in range(8):
                MM(bank.ap(0, 128, m * 2, m * 2 + 2), slot[:, k, m * 128:(m + 1) * 128], silc[:, k, :],
                   k == 0, k == 7, [sk, "silc"], [bank.key])
        TT(modT[:, l, j * 8:(j + 1) * 8, :], bank.ap(0, 128, 0, 16).rearrange("p (m w) -> p m w", w=2),
           bc(bT[:, l, j * 8:(j + 1) * 8], 2, [128, 8, 2]), ALU.add, [bank.key, "bT"], [("modT", l, j)])

    def adaln_A(l, which):
        A, g, vi = ((A1, gmix, 1), (A2, gmlp, 4))[which]
        TS(A[:, l, :, :], modT[:, l, vi * 8:(vi + 1) * 8, :], 1.0, 32.0, ALU.add, ALU.mult, [("modT", l, vi)], [("A", l, which)])
        TT(A[:, l, :, :], A[:, l, :, :], bc(g[:, l, :], 2, [128, 8, 2]), ALU.mult, [("A", l, which), "gmix", "gmlp"],
           [("A", l, which)])

    def mod(l, vi, c, w):
        return modT[:, l, vi * 8 + c, w:w + 1]

    def norm_pass(l, A_of, sh_of, blocks, dst, tmp_pool, A_keys):
        sq, rs, tmpf = tmp_pool
        for bi in blocks:
            t0, n = BLK[bi]
            w = 1 if bi == 0 else 0
            s = sq[bi % 2]
            sk = ("sq", bi % 2)
            ACT(s[:, :, 0:n], X[:, :, t0:t0 + n], AF.Square, [xk(c, bi) for c in range(8)], [sk])
            bank = next_bank()
            for c in range(8):
                MM(bank.ap(0, 128, 0, n), ones[:, :], s[:, c, 0:n], c == 0, c == 7, [sk, "ones"], [bank.key])
            r_ = rs[bi % 2]
            rk_ = ("rs", bi % 2)
            ACT(r_[:, 0:n], bank.ap(0, 128, 0, n), AF.Ln, [bank.key, "epsb"], [rk_], bias=epsb[:, 0:1])
            ACT(r_[:, 0:n], r_[:, 0:n], AF.Exp, [rk_], [rk_], scale=-0.5)
            for c in range(8):
                out_ap, out_key, shv = dst(c, bi, t0, n)
                if shv is None:
                    STT(out_ap, X[:, c, t0:t0 + n], A_of(c, w), r_[:, 0:n], ALU.mult, ALU.mult,
                        [xk(c, bi), rk_] + A_keys, [out_key])
                else:
                    tf = tmpf[c % 2]
                    tk = ("tmpf", c % 2)
                    STT(tf[:, 0:n], X[:, c, t0:t0 + n], A_of(c, w), r_[:, 0:n], ALU.mult, ALU.mult,
                        [xk(c, bi), rk_] + A_keys, [tk])
                    ACT(out_ap, tf[:, 0:n], AF.Identity, [tk] + A_keys, [out_key], bias=shv(c, w))

    def xupdate(l, gvi, oc, bi, t0, n, bank):
        w = 1 if bi == 0 else 0
        STT(X[:, oc, t0:t0 + n], bank.ap(0, 128, 0, n), mod(l, gvi, oc, w), X[:, oc, t0:t0 + n],
            ALU.mult, ALU.add, [bank.key, ("modT", l, gvi), xk(oc, bi)], [xk(oc, bi)])

    B_ORDER = [1, 0] + list(range(17, 1, -1))
    fin_state = {"cnt": 0, "outs": [], "done": False}

    def final_norm_block(bi, sq, rs, ost):
        t0, n = BLK[bi]
        s_ = sq[bi % 2]
        sk = ("sq", bi % 2)
        ACT(s_[:, :, 0:n], X[:, :, t0:t0 + n], AF.Square, [xk(c, bi) for c in range(8)], [sk])
        bank = next_bank()
        for c in range(8):
            MM(bank.ap(0, 128, 0, n), ones[:, :], s_[:, c, 0:n], c == 0, c == 7, [sk, "ones"], [bank.key])
        r_ = rs[bi % 2]
        rk_ = ("rs", bi % 2)
        ACT(r_[:, 0:n], bank.ap(0, 128, 0, n), AF.Ln, [bank.key, "epsb"], [rk_], bias=epsb[:, 0:1])
        ACT(r_[:, 0:n], r_[:, 0:n], AF.Exp, [rk_], [rk_], scale=-0.5)
        for c in range(8):
            i = fin_state["cnt"] % 4
            fin_state["cnt"] += 1
            STT(ost[i][:, 0:n], X[:, c, t0:t0 + n], AF_[:, c:c + 1], r_[:, 0:n], ALU.mult, ALU.mult,
                [xk(c, bi), rk_, "AFin"], [("ost", i)])
            ok = ("out", c, bi)
            fw.dma("sp", outT_d[c * 128:(c + 1) * 128, t0 - 256:t0 - 256 + n], ost[i][:, 0:n], reads=[("ost", i)], writes=[ok])
            fin_state["outs"].append(ok)


    def chk(k, l):
        if stage < k + 10 * l:
            raise _Stop()

    try:
      for l in range(2):
          qblocks = list(range(5)) if l == 0 else [1, 2, 3, 4]
          chk(2, l)
          def _p0():
              m0 = ar.mark()
              if l == 0:
                  wada0 = [ar.alloc("wada", [128, 8, 1024], BF16) for _ in range(2)]
                  adaln_piece(0, 0, wada0, 4)
                  adaln_piece(0, 1, wada0, 5)
                  adaln_A(0, 0)
              sq = [ar.alloc("sq", [128, 8, 512], BF16) for _ in range(2)]
              rs = [ar.alloc("rs", [128, 512], F32) for _ in range(2)]
              tmpf = [ar.alloc("tmpf", [128, 512], F32) for _ in range(2)]
              norm_pass(l, lambda c, w: A1[:, l, c, w:w + 1], lambda c, w: mod(l, 0, c, w), range(5),
                        lambda c, bi, t0, n: (H[:, c, t0:t0 + n], hk(bi), (lambda c_, w_: mod(l, 0, c_, w_))),
                        (sq, rs, tmpf), [("A", l, 0), ("modT", l, 0)])
              if l == 0:
                  for j in range(2, 6):
                      adaln_piece(0, j, wada0, 4 + j % 2)
                  adaln_A(0, 1)
              ar.release(m0)
          if l == 0:
              _p0()

          m0 = ar.mark()
          Wret = ar.alloc("Wret", [128, 8, 1024], BF16)
          fw.dma("pool", Wret[:, :, :], w_in_d[l].rearrange("(k p) n -> p k n", p=128)[:, :, 0:1024], writes=["Wret"])
          Wout_r = ar.alloc("Wout_r", [128, 2, 1024], BF16)
          fw.dma("pool", Wout_r[:, :, :], w_out_d[l, 0:256, :].rearrange("(k p) n -> p k n", p=128), writes=["Wout_r"])
          cosD = ar.alloc("cosD", [128, 16, 64], F32)
          sinS = ar.alloc("sinS", [128, 16, 64], F32)
          fw.dma("sp", cosD[:, :, :], cosD_d.rearrange("p (n d) -> p n d", d=64), writes=["cosD"])
          fw.dma("sp", sinS[:, :, :], sinS_d.rearrange("p (n d) -> p n d", d=64), writes=["sinS"])
          mconst = ar.alloc("mconst", [128, 4, 128], F32)
          fw.dma("sp", mconst[:, :, :], mconst_d.rearrange("p (a i) -> p a i", a=4), writes=["mconst"])
          efree = ar.alloc("efree", [128, 2, 128], F32)
          fw.dma("sp", efree[:, :, :], efree_d.rearrange("p (a i) -> p a i", a=2), writes=["efree"])
          Mtab = ar.alloc("Mtab", [128, 4, 128], F32)
          mtmp = ar.alloc("mtmp", [128, 2, 128], F32)
          tabQ = ar.alloc("tabQ", [128, 2, 4, 128], F32)
          wk = ar.alloc("wk", [128, 2, 4], F32)
          G = ar.alloc("G", [128, 2, 4], F32)
          snapB = ar.alloc("snapB", [64, NT, 256], BF16)
          Sst = ar.alloc("Sst", [64, 256], F32)
          Sbf = ar.alloc("Sbf", [64, 256], BF16)
          for d_ in range(2):
              ACT(wk[:, d_, :], lg[:, l, d_, :], AF.Exp, ["lg", "ecol"], ["wk"], scale=ecol[:, d_:d_ + 1])
          TS(wk[:, :, :], wk[:, :, :], 0.125, None, ALU.mult, None, ["wk"], ["wk"])
          ACT(G[:, :, :], lg[:, l, :, :], AF.Exp, ["lg"], ["G"], scale=128.0)
          for h in range(4):
              ACT(mtmp[:, 0, :], mconst[:, 0, :], AF.Exp, ["mconst", "lg"], ["mtmp0"], scale=lg[:, l, 0, h:h + 1])
              TT(mtmp[:, 0, :], mtmp[:, 0, :], mconst[:, 2, :], ALU.mult, ["mtmp0", "mconst"], ["mtmp0"])
              ACT(mtmp[:, 1, :], mconst[:, 1, :], AF.Exp, ["mconst", "lg"], ["mtmp1"], scale=lg[:, l, 1, h:h + 1])
              TT(mtmp[:, 1, :], mtmp[:, 1, :], mconst[:, 3, :], ALU.mult, ["mtmp1", "mconst"], ["mtmp1"])
              TT(Mtab[:, h, :], mtmp[:, 0, :], mtmp[:, 1, :], ALU.add, ["mtmp0", "mtmp1"], ["Mtab"])
              for d_ in range(2):
                  ACT(tabQ[:, d_, h, :], efree[:, d_, :], AF.Exp, ["efree", "lg"], ["tabQ"], scale=lg[:, l, d_, h:h + 1])

          m1 = ar.mark()
          krf = [ar.alloc("krf", [128, 512], F32) for _ in range(2)]
          t1f = [ar.alloc("t1f", [128, 512], F32) for _ in range(2)]
          t2f = [ar.alloc("t2f", [128, 512], F32) for _ in range(2)]
          kdt = [ar.alloc("kdt", [128, 256], BF16) for _ in range(2)]
          vbt = [ar.alloc("vbt", [128, 256], BF16) for _ in range(2)]

          def rope_tok(src_ap, src_key, tb, out_ap, out_key, i2, nh):
              W = nh * 64
              if tb < 2:
                  CP(out_ap, src_ap, [src_key], [out_key])
                  return
              n_ = tb - 2
              t1, t2 = t1f[i2], t2f[i2]
              k1, k2 = ("t1f", i2), ("t2f", i2)
              s3 = src_ap.rearrange("p (h d) -> p h d", h=nh)
              s4 = src_ap.rearrange("p (h a d) -> p h a d", h=nh, a=2)
              TT(t1[:, 0:W].rearrange("p (h d) -> p h d", h=nh), s3, bc(cosD[:, n_, :], 1, [128, nh, 64]), ALU.mult,
                 [src_key, "cosD"], [k1])
              t24 = t2[:, 0:W].rearrange("p (h a d) -> p h a d", h=nh, a=2)
              TT(t24[:, :, 0, :], s4[:, :, 1, :], bc(sinS[:, n_, 0:32], 1, [128, nh, 32]), ALU.mult,
                 [src_key, "sinS"], [k2])
              TT(t24[:, :, 1, :], s4[:, :, 0, :], bc(sinS[:, n_, 32:64], 1, [128, nh, 32]), ALU.mult,
                 [src_key, "sinS"], [k2])
              TT(out_ap, t1[:, 0:W], t2[:, 0:W], ALU.add, [k1, k2], [out_key])

          chk(3, l)
          MEMSET(Sst[:, :], 0.0, ["Sst"], eng="dve")
          p2 = {}

          def p2A(it):
              tb = B_ORDER[it]
              i2 = it % 2
              bi = 0 if tb < 2 else 1 + (tb - 2) // 4
              bank = next_bank()
              for k in range(8):
                  MM(bank.ap(0, 128, 0, 512), H[:, k, tb * 128:(tb + 1) * 128], Wret[:, k, 256:768], k == 0, k == 7,
                     [hk(bi), "Wret"], [bank.key])
              rope_tok(bank.ap(0, 128, 0, 256), bank.key, tb, krf[i2][:, 0:256], ("krf", i2), i2, 4)
              TT(kdt[i2][:, :].rearrange("p (h d) -> p h d", h=4), krf[i2][:, 0:256].rearrange("p (h d) -> p h d", h=4),
                 bc(wk[:, 1, :], 2, [128, 4, 64]), ALU.mult, [("krf", i2), "wk"], [("kdt", i2)])
              CP(vbt[i2][:, :], bank.ap(0, 128, 256, 512), [bank.key], [("vbt", i2)], eng="act")

          def p2B(it):
              tb = B_ORDER[it]
              i2 = it % 2
              ub = next_bank()
              for h in range(4):
                  MM(ub.ap(0, 64, h * 64, h * 64 + 64), kdt[i2][:, h * 64:(h + 1) * 64], vbt[i2][:, h * 64:(h + 1) * 64],
                     True, True, [("kdt", i2), ("vbt", i2)], [ub.key])
              CP(snapB[:, tb, :], Sst[:, :], ["Sst"], [("snapB", tb)], eng="act")
              TT(Sst[:, :].rearrange("p (h d) -> p h d", h=4), Sst[:, :].rearrange("p (h d) -> p h d", h=4),
                 bc(G[0:64, 1, :], 2, [64, 4, 64]), ALU.mult, ["Sst", "G"], ["Sst"])
              TT(Sst[:, :], Sst[:, :], ub.ap(0, 64, 0, 256), ALU.add, ["Sst", ub.key], ["Sst"])

          p2A(0)
          for it in range(NT):
              if it + 1 < NT:
                  p2A(it + 1)
              p2B(it)

          chk(4, l)
          qkb = [ar.alloc("qkb", [128, 512], BF16) for _ in range(2)]
          qT = [ar.alloc("qT", [64, 4, 128], BF16) for _ in range(2)]
          kT = [ar.alloc("kT", [64, 4, 128], BF16) for _ in range(2)]
          qfT = [ar.alloc("qfT", [64, 4, 128], BF16) for _ in range(2)]
          qbT = [ar.alloc("qbT", [64, 4, 128], BF16) for _ in range(2)]
          SD = [ar.alloc("SD", [128, 4, 128], BF16) for _ in range(2)]
          st4 = [ar.alloc("st4", [128, 16], F32) for _ in range(2)]
          cen = [ar.alloc("cen", [128, 256], F32) for _ in range(2)]
          gtf = [ar.alloc("gtf", [128, 256], F32) for _ in range(3)]
          rout = [ar.alloc("rout", [128, 256], BF16) for _ in range(2)]
          mixr = [ar.alloc("mixr", [128, 2, 512], BF16) for _ in range(2)]
          MEMSET(Sst[:, :], 0.0, ["Sst"], eng="dve")
          MEMSET(Sbf[:, :], 0.0, ["Sbf"], eng="dve")
          p3 = {}
          p3o = {}

          def p3A(tb):
              i2 = tb % 2
              bi = 0 if tb < 2 else 1 + (tb - 2) // 4
              want_out = bi in qblocks
              qk = Bank(*gen_banks[2 * (tb % 2)])
              vg = Bank(*gen_banks[2 * (tb % 2) + 1])
              p3[tb] = (qk, vg)
              for k in range(8):
                  MM(qk.ap(0, 128, 0, 512), H[:, k, tb * 128:(tb + 1) * 128], Wret[:, k, 0:512], k == 0, k == 7,
                     [hk(bi), "Wret"], [qk.key])
              for k in range(8):
                  MM(vg.ap(0, 128, 0, 512), H[:, k, tb * 128:(tb + 1) * 128], Wret[:, k, 512:1024], k == 0, k == 7,
                     [hk(bi), "Wret"], [vg.key])

          def p3A2(tb):
              i2 = tb % 2
              bi = 0 if tb < 2 else 1 + (tb - 2) // 4
              want_out = bi in qblocks
              qk, vg = p3[tb]
              if want_out:
                  rope_tok(qk.ap(0, 128, 0, 512), qk.key, tb, krf[i2][:, 0:512], ("krf", i2), i2, 8)
              else:
                  rope_tok(qk.ap(0, 128, 256, 512), qk.key, tb, krf[i2][:, 256:512], ("krf", i2), i2, 4)
              TT(kdt[i2][:, :].rearrange("p (h d) -> p h d", h=4), krf[i2][:, 256:512].rearrange("p (h d) -> p h d", h=4),
                 bc(wk[:, 0, :], 2, [128, 4, 64]), ALU.mult, [("krf", i2), "wk"], [("kdt", i2)])
              CP(vbt[i2][:, :], vg.ap(0, 128, 0, 256), [vg.key], [("vbt", i2)], eng="act")
              if want_out:
                  CP(qkb[i2][:, :], krf[i2][:, :], [("krf", i2)], [("qkb", i2)], eng="act")
                  i3 = tb % 3
                  gk = ("gtf", i3)
                  ACT(gtf[i3][:, :], vg.ap(0, 128, 256, 512), AF.Exp, [vg.key], [gk], scale=-1.0)
                  ACT(gtf[i3][:, :], gtf[i3][:, :], AF.Ln, [gk, "epsb"], [gk], bias=epsb[:, 2:3])
                  ACT(gtf[i3][:, :], gtf[i3][:, :], AF.Exp, [gk], [gk], scale=-1.0)
                  TT(gtf[i3][:, :], vg.ap(0, 128, 256, 512), gtf[i3][:, :], ALU.mult, [vg.key, gk], [gk])

          def p3B(tb):
              i2 = tb % 2
              bi = 0 if tb < 2 else 1 + (tb - 2) // 4
              t0b, nb = BLK[bi]
              want_out = bi in qblocks
              if want_out:
                  for h in range(4):
                      TR(psT[0:64, h * 128:(h + 1) * 128], qkb[i2][:, h * 64:(h + 1) * 64], [("qkb", i2)], ["psT"])
                      TR(psT[0:64, (4 + h) * 128:(5 + h) * 128], qkb[i2][:, 256 + h * 64:256 + (h + 1) * 64], [("qkb", i2)], ["psT"])
                  pq = psT[0:64, 0:512].rearrange("p (h i) -> p h i", h=4)
                  pk = psT[0:64, 512:1024].rearrange("p (h i) -> p h i", h=4)
                  CP(qT[i2][:, :, :], pq, ["psT"], [("qT", i2)], eng="act")
                  CP(kT[i2][:, :, :], pk, ["psT"], [("kT", i2)], eng="act")
                  TT(qfT[i2][:, :, :], pq, tabQ[0:64, 0, :, :], ALU.mult, ["psT", "tabQ"], [("qfT", i2)])
                  TT(qbT[i2][:, :, :], pq, tabQ[0:64, 1, :, :], ALU.mult, ["psT", "tabQ"], [("qbT", i2)])
                  stb = Bank(*gen_banks[4])
                  for h in range(4):
                      MM(stb.ap(0, 128, h * 128, (h + 1) * 128), kT[i2][:, h, :], qT[i2][:, h, :], True, True,
                         [("kT", i2), ("qT", i2)], [stb.key])
                  TT(SD[i2][:, :, :], stb.ap(0, 128, 0, 512).rearrange("p (h i) -> p h i", h=4), Mtab[:, :, :], ALU.mult,
                     [stb.key, "Mtab"], [("SD", i2)])

          def p3B2(tb):
              i2 = tb % 2
              bi = 0 if tb < 2 else 1 + (tb - 2) // 4
              t0b, nb = BLK[bi]
              want_out = bi in qblocks
              if want_out:
                  ob = Bank(*gen_banks[5 + (tb % 2)])
                  p3o[tb] = ob
                  for h in range(4):
                      oc_ = ob.ap(0, 128, h * 64, (h + 1) * 64)
                      MM(oc_, qfT[i2][:, h, :], Sbf[:, h * 64:(h + 1) * 64], True, False, [("qfT", i2), "Sbf"], [ob.key])
                      MM(oc_, qbT[i2][:, h, :], snapB[:, tb, h * 64:(h + 1) * 64], False, False,
                         [("qbT", i2), ("snapB", tb)], [ob.key])
                      MM(oc_, SD[i2][:, h, :], vbt[i2][:, h * 64:(h + 1) * 64], False, True,
                         [("SD", i2), ("vbt", i2)], [ob.key])
              ub = Bank(*gen_banks[4])
              for h in range(4):
                  MM(ub.ap(0, 64, h * 64, h * 64 + 64), kdt[i2][:, h * 64:(h + 1) * 64], vbt[i2][:, h * 64:(h + 1) * 64],
                     True, True, [("kdt", i2), ("vbt", i2)], [ub.key])
              TT(Sst[:, :].rearrange("p (h d) -> p h d", h=4), Sst[:, :].rearrange("p (h d) -> p h d", h=4),
                 bc(G[0:64, 0, :], 2, [64, 4, 64]), ALU.mult, ["Sst", "G"], ["Sst"])
              TT(Sst[:, :], Sst[:, :], ub.ap(0, 64, 0, 256), ALU.add, ["Sst", ub.key], ["Sst"])
              CP(Sbf[:, :], Sst[:, :], ["Sst"], ["Sbf"], eng="act")

          def p3C(tb):
              i2 = tb % 2
              bi = 0 if tb < 2 else 1 + (tb - 2) // 4
              t0b, nb = BLK[bi]
              if bi not in qblocks:
                  return
              ob = p3o.pop(tb)
              s4_ = st4[i2]
              sk4 = ("st4", i2)
              o3 = ob.ap(0, 128, 0, 256).rearrange("p (h d) -> p h d", h=4)
              RSUM(s4_[:, 0:4], o3, [ob.key], [sk4])
              TS(s4_[:, 0:4], s4_[:, 0:4], -1.0 / 64.0, None, ALU.mult, None, [sk4], [sk4])
              c3 = cen[i2][:, :].rearrange("p (h d) -> p h d", h=4)
              TT(c3, o3, bc(s4_[:, 0:4], 2, [128, 4, 64]), ALU.add, [ob.key, sk4], [("cen", i2)])
              sq1 = mtmp[:, :, :].rearrange("p a i -> p (a i)")
              TT(sq1, cen[i2][:, :], cen[i2][:, :], ALU.mult, [("cen", i2)], ["sqf1"])
              RSUM(s4_[:, 4:8], sq1.rearrange("p (h d) -> p h d", h=4), ["sqf1"], [sk4])
              ACT(s4_[:, 4:8], s4_[:, 4:8], AF.Ln, [sk4, "epsb"], [sk4], bias=epsb[:, 1:2], scale=1.0 / 64.0)
              ACT(s4_[:, 4:8], s4_[:, 4:8], AF.Exp, [sk4], [sk4], scale=-0.5)
              i3 = tb % 3
              gk = ("gtf", i3)
              TT(c3, c3, bc(s4_[:, 4:8], 2, [128, 4, 64]), ALU.mult, [("cen", i2), sk4], [("cen", i2)])
              TT(rout[i2][:, :], cen[i2][:, :], gtf[i3][:, :], ALU.mult, [("cen", i2), gk], [("rout", i2)])

          def p3Cp(tb):
              i2 = tb % 2
              bi = 0 if tb < 2 else 1 + (tb - 2) // 4
              t0b, nb = BLK[bi]
              if bi not in qblocks:
                  return
              mb = mixr[bi % 2]
              mk_ = ("mixr", bi % 2)
              lt = (tb * 128 - t0b) // 128
              for p in range(2):
                  TR(psT[:, p * 128:(p + 1) * 128], rout[i2][:, p * 128:(p + 1) * 128], [("rout", i2)], ["psT"])
              CP(mb[:, :, lt * 128:(lt + 1) * 128], psT[:, 0:256].rearrange("p (a i) -> p a i", a=2), ["psT"], [mk_], eng="act")
              if tb * 128 + 128 == t0b + nb:
                  for oc in range(8):
                      bank = next_bank([2 * ((tb + 1) % 2), 2 * ((tb + 1) % 2) + 1, 4])
                      for p in range(2):
                          MM(bank.ap(0, 128, 0, nb), Wout_r[:, p, oc * 128:(oc + 1) * 128], mb[:, p, 0:nb], p == 0, p == 1,
                             ["Wout_r", mk_], [bank.key])
                      xupdate(l, 2, oc, bi, t0b, nb, bank)

          p3A(0)
          p3A2(0)
          for tb in range(NT):
              if tb >= 1:
                  p3C(tb - 1)
              if tb + 1 < NT:
                  p3A(tb + 1)
              p3B(tb)
              if tb + 1 < NT:
                  p3A2(tb + 1)
              p3B2(tb)
              if tb >= 1:
                  p3Cp(tb - 1)
          p3C(NT - 1)
          p3Cp(NT - 1)
          ar.release(m0)

          chk(5, l)
          m0 = ar.mark()
          Wu = ar.alloc("Wu", [128, 8, 256], BF16)
          fw.dma("pool", Wu[:, :, :], w_in_d[l].rearrange("(k p) n -> p k n", p=128)[:, :, 1440:1696], writes=["Wu"])
          Wpl = ar.alloc("Wpl", [64, 4, 64], BF16)
          fw.dma("pool", Wpl[:, :, :], w_pool_d[l].rearrange("g c d -> c g d"), writes=["Wpl"])
          Wout_p = ar.alloc("Wout_p", [64, 4, 1024], BF16)
          fw.dma("pool", Wout_p[:, :, :], w_out_d[l, 768:1024, :].rearrange("(g c) n -> c g n", c=64), writes=["Wout_p"])
          bands = ar.alloc("bands", [128, 20, 128], BF16)
          fw.dma("pool", bands[:, :, :], bands_d.rearrange("p (a i) -> p a i", a=20), writes=["bands"])
          utok = ar.alloc("utok", [128, NT, 256], BF16)
          pooledT = [ar.alloc("pooledT", [64, 4, 128], BF16) for _ in range(2)]
          mixp = [ar.alloc("mixp", [64, 4, 512], BF16) for _ in range(2)]
          tiles = list(range(NT)) if l == 0 else list(range(2, NT))
          for tb in tiles:
              bi = 0 if tb < 2 else 1 + (tb - 2) // 4
              bank = next_bank()
              for k in range(8):
                  MM(bank.ap(0, 128, 0, 256), H[:, k, tb * 128:(tb + 1) * 128], Wu[:, k, :], k == 0, k == 7,
                     [hk(bi), "Wu"], [bank.key])
              CP(utok[:, tb, :], bank.ap(0, 128, 0, 256), [bank.key], [("utok", tb)], eng="act")
          for tb in tiles:
              i2 = tb % 2
              bi = 0 if tb < 2 else 1 + (tb - 2) // 4
              t0b, nb = BLK[bi]
              s0, s1 = (0, 1) if tb < 2 else (2, NT - 1)
              first, last = tb == s0, tb == s1
              pb = next_bank()
              for g in range(4):
                  srcs = []
                  if not first:
                      srcs.append((tb - 1, g * 5 + 3))
                  srcs.append((tb, g * 5 + (1 if first else (2 if last else 0))))
                  if not last:
                      srcs.append((tb + 1, g * 5 + 4))
                  for si, (src, bidx) in enumerate(srcs):
                      MM(pb.ap(0, 64, g * 128, (g + 1) * 128), utok[:, src, g * 64:(g + 1) * 64], bands[:, bidx, :],
                         si == 0, si == len(srcs) - 1, [("utok", src), "bands"], [pb.key])
              CP(pooledT[i2][:, :, :], pb.ap(0, 64, 0, 512).rearrange("p (g i) -> p g i", g=4), [pb.key], [("pooledT", i2)],
                 eng="act")
              yb = next_bank()
              for g in range(4):
                  MM(yb.ap(0, 64, g * 128, (g + 1) * 128), Wpl[:, g, :], pooledT[i2][:, g, :], True, True,
                     ["Wpl", ("pooledT", i2)], [yb.key])
              mb = mixp[bi % 2]
              mk_ = ("mixp", bi % 2)
              lt = (tb * 128 - t0b) // 128
              TT(mb[:, :, lt * 128:(lt + 1) * 128], yb.ap(0, 64, 0, 512).rearrange("p (g i) -> p g i", g=4),
                 bc(psc[:, l, :], 2, [64, 4, 128]), ALU.mult, [yb.key, "psc"], [mk_])
              if tb * 128 + 128 == t0b + nb:
                  for oc in range(8):
                      bank = next_bank()
                      for g in range(4):
                          MM(bank.ap(0, 128, 0, nb), Wout_p[:, g, oc * 128:(oc + 1) * 128], mb[:, g, 0:nb], g == 0, g == 3,
                             ["Wout_p", mk_], [bank.key])
                      xupdate(l, 2, oc, bi, t0b, nb, bank)
          ar.release(m0)

          chk(6, l)
          m0 = ar.mark()
          cqn = ar.alloc("cqn", [128, 2, T], BF16)
          ckvn = ar.alloc("ckvn", [128, T], BF16)
          kpe96 = ar.alloc("kpe96", [128, T], BF16)
          ropeM = ar.alloc("ropeM", [128, 2, 2048], F32)
          fw.dma("sp", ropeM[:, :, :], ropeM_d.rearrange("p (a t) -> p a t", a=2), writes=["ropeM"])
          rt1 = [ar.alloc("rt1", [128, 512], F32) for _ in range(2)]
          rt2 = [ar.alloc("rt2", [128, 512], F32) for _ in range(2)]
          m1 = ar.mark()
          Wm1 = ar.alloc("Wm1", [128, 8, 416], BF16)
          Wm2 = ar.alloc("Wm2", [128, 8, 96], BF16)
          fw.dma("pool", Wm1[:, :, :], w_in_d[l].rearrange("(k p) n -> p k n", p=128)[:, :, 1024:1440], writes=["Wm1"])
          fw.dma("pool", Wm2[:, :, :], w_in_d[l].rearrange("(k p) n -> p k n", p=128)[:, :, 1632:1728], writes=["Wm2"])
          sqc = [ar.alloc("sqc", [128, 3, 512], BF16) for _ in range(2)]
          rsc = [ar.alloc("rsc", [128, 2, 512], F32) for _ in range(2)]
          for bi in range(5):
              t0, n = BLK[bi]
              i2 = bi % 2
              need_q = bi in qblocks
              cb = []
              mlist = ([0, 1] if need_q else []) + [2]
              for m in mlist:
                  bank = next_bank()
                  for k in range(8):
                      MM(bank.ap(0, 128, 0, n), Wm1[:, k, m * 128:(m + 1) * 128], H[:, k, t0:t0 + n], k == 0, k == 7,
                         ["Wm1", hk(bi)], [bank.key])
                  ACT(sqc[i2][:, m, 0:n], bank.ap(0, 128, 0, n), AF.Square, [bank.key], [("sqc", i2, m)])
                  cb.append((m, bank))
              if need_q:
                  sb = next_bank()
                  for m in range(2):
                      MM(sb.ap(0, 128, 0, n), ones[:, :], sqc[i2][:, m, 0:n], m == 0, m == 1, ["ones", ("sqc", i2, m)], [sb.key])
                  ACT(rsc[i2][:, 0, 0:n], sb.ap(0, 128, 0, n), AF.Ln, [sb.key, "epsb"], [("rsc", i2, 0)], bias=epsb[:, 1:2],
                      scale=1.0 / 256.0)
                  ACT(rsc[i2][:, 0, 0:n], rsc[i2][:, 0, 0:n], AF.Exp, [("rsc", i2, 0)], [("rsc", i2, 0)], scale=-0.5)
              sb2 = next_bank()
              MM(sb2.ap(0, 128, 0, n), ones[:, :], sqc[i2][:, 2, 0:n], True, True, ["ones", ("sqc", i2, 2)], [sb2.key])
              ACT(rsc[i2][:, 1, 0:n], sb2.ap(0, 128, 0, n), AF.Ln, [sb2.key, "epsb"], [("rsc", i2, 1)], bias=epsb[:, 1:2],
                  scale=1.0 / 128.0)
              ACT(rsc[i2][:, 1, 0:n], rsc[i2][:, 1, 0:n], AF.Exp, [("rsc", i2, 1)], [("rsc", i2, 1)], scale=-0.5)
              for (m, bank) in cb:
                  if m < 2:
                      STT(cqn[:, m, t0:t0 + n], bank.ap(0, 128, 0, n), qn[:, l, m:m + 1], rsc[i2][:, 0, 0:n], ALU.mult, ALU.mult,
                          [bank.key, "qn", ("rsc", i2, 0)], [("cqn", bi)])
                  else:
                      STT(ckvn[:, t0:t0 + n], bank.ap(0, 128, 0, n), kvn[:, l:l + 1], rsc[i2][:, 1, 0:n], ALU.mult, ALU.mult,
                          [bank.key, "kvn", ("rsc", i2, 1)], [("ckvn", bi)])
              kp = next_bank()
              for k in range(8):
                  MM(kp.ap(0, 96, 0, n), Wm1[:, k, 320:416], H[:, k, t0:t0 + n], k == 0, k == 7, ["Wm1", hk(bi)], [kp.key])
              if bi == 0:
                  CP(kpe96[64:96, t0:t0 + n], kp.ap(64, 96, 0, n), [kp.key], [("kpe96", bi)])
              else:
                  ks = next_bank()
                  for k in range(8):
                      MM(ks.ap(0, 96, 0, n), Wm2[:, k, 0:96], H[:, k, t0:t0 + n], k == 0, k == 7, ["Wm2", hk(bi)], [ks.key])
                  lt0 = t0 - 256
                  TT(rt1[i2][64:96, 0:n], kp.ap(64, 96, 0, n), ropeM[64:96, 0, lt0:lt0 + n], ALU.mult, [kp.key, "ropeM"],
                     [("rt1", i2)])
                  TT(rt2[i2][64:96, 0:n], ks.ap(64, 96, 0, n), ropeM[64:96, 1, lt0:lt0 + n], ALU.mult, [ks.key, "ropeM"],
                     [("rt2", i2)])
                  TT(kpe96[64:96, t0:t0 + n], rt1[i2][64:96, 0:n], rt2[i2][64:96, 0:n], ALU.add, [("rt1", i2), ("rt2", i2)],
                     [("kpe96", bi)])
          ar.release(m1)

          chk(7, l)
          Wuq = ar.alloc("Wuq", [128, 2, 1536], BF16)
          fw.dma("pool", Wuq[:, :, :], w_uq_d[l].rearrange("(k p) n -> p k n", p=128), writes=["Wuq"])
          Wukv = ar.alloc("Wukv", [128, 1024], BF16)
          fw.dma("pool", Wukv[:, :], w_ukv_d[l], writes=["Wukv"])
          Wout_m = ar.alloc("Wout_m", [64, 8, 1024], BF16)
          fw.dma("pool", Wout_m[:, :, :], w_out_d[l, 256:768, :].rearrange("(h c) n -> c h n", c=64), writes=["Wout_m"])
          kTm = ar.alloc_at("kTm", [128, 4, T], BF16, H_OFF)
          vaug = ar.alloc_at("vaug", [128, NT, 4, 65], BF16, H_OFF + 4 * T * 2)
          qTh = [ar.alloc("qTh", [128, 512], BF16) for _ in range(8)]
          PT = [ar.alloc("PT", [128, 1024], BF16) for _ in range(3)]
          rden = [ar.alloc("rden", [128, 512], F32) for _ in range(2)]
          bcs = [ar.alloc("bcs", [64, 512], F32) for _ in range(2)]
          mixh = [ar.alloc("mixh", [64, 4, 512], BF16) for _ in range(2)]
          qi = 0
          pti_box = [0]
          oi_box = [0]
          for hh in range(2):
              MEMSET(vaug[:, :, :, 64:65], 1.0, [("vaug", kt) for kt in range(NT)], eng="dve")
              for hl in range(4):
                  h = hh * 4 + hl
                  for bi in range(5):
                      t0, n = BLK[bi]
                      bank = next_bank([0, 1, 2, 3, 6])
                      MM(bank.ap(0, 64, 0, n), Wukv[:, h * 128:h * 128 + 64], ckvn[:, t0:t0 + n], True, True,
                         ["Wukv", ("ckvn", bi)], [bank.key])
                      CP(kTm[0:64, hl, t0:t0 + n], bank.ap(0, 64, 0, n), [bank.key], [("kTm", hl, bi)],
                         eng=("act" if bi % 2 else "dve"))
                      CP(kTm[64:96, hl, t0:t0 + n], kpe96[64:96, t0:t0 + n], [("kpe96", bi)], [("kTm", hl, bi)], eng="dve")
              wv = Wukv[:, :].rearrange("p (h e) -> p h e", h=8)
              for kt in range(NT):
                  bi = 0 if kt < 2 else 1 + (kt - 2) // 4
                  bank = next_bank([0, 1, 2, 3, 6])
                  for hl in range(4):
                      MM(bank.ap(0, 128, hl * 64, hl * 64 + 64), ckvn[:, kt * 128:(kt + 1) * 128],
                         wv[:, hh * 4 + hl, 64:128], True, True, ["Wukv", ("ckvn", bi)], [bank.key])
                  CP(vaug[:, kt, :, 0:64], bank.ap(0, 128, 0, 256).rearrange("p (h e) -> p h e", h=4), [bank.key],
                     [("vaug", kt)], eng=("act" if kt % 2 else "dve"))
              def qproj_items(bi, qset):
                  t0, n = BLK[bi]
                  items = []
                  for hl in range(4):
                      def item(hl=hl):
                          h = hh * 4 + hl
                          q_ = qTh[qset * 4 + hl]
                          qk_ = ("qTh", qset * 4 + hl)
                          qa = next_bank([6, 7])
                          for k in range(2):
                              MM(qa.ap(0, 96, 0, n), Wuq[:, k, h * 192:h * 192 + 96], cqn[:, k, t0:t0 + n], k == 0, k == 1,
                                 ["Wuq", ("cqn", bi)], [qa.key])
                          CP(q_[0:64, 0:n], qa.ap(0, 64, 0, n), [qa.key], [qk_], eng="dve")
                          if bi == 0:
                              CP(q_[64:96, 0:n], qa.ap(64, 96, 0, n), [qa.key], [qk_], eng="dve")
                          else:
                              lt0 = t0 - 256
                              i2 = hl % 2
                              TT(rt1[i2][64:96, 0:n], qa.ap(64, 96, 0, n), ropeM[64:96, 0, lt0:lt0 + n], ALU.mult,
                                 [qa.key, "ropeM"], [("rt1", i2)])
                              qb_ = next_bank([6, 7])
                              for k in range(2):
                                  MM(qb_.ap(0, 96, 0, n), Wuq[:, k, h * 192 + 96:h * 192 + 192], cqn[:, k, t0:t0 + n], k == 0,
                                     k == 1, ["Wuq", ("cqn", bi)], [qb_.key])
                              TT(rt2[i2][64:96, 0:n], qb_.ap(64, 96, 0, n), ropeM[64:96, 1, lt0:lt0 + n], ALU.mult,
                                 [qb_.key, "ropeM"], [("rt2", i2)])
                              TT(q_[64:96, 0:n], rt1[i2][64:96, 0:n], rt2[i2][64:96, 0:n], ALU.add,
                                 [("rt1", i2), ("rt2", i2)], [qk_])
                      items.append(item)
                  return items

              def wout_items(bi, mslot):
                  t0, n = BLK[bi]
                  mb = mixh[mslot]
                  items = []
                  for oc in range(8):
                      def item(oc=oc):
                          bank = next_bank([6, 7])
                          for hl in range(4):
                              MM(bank.ap(0, 128, 0, n), Wout_m[:, hh * 4 + hl, oc * 128:(oc + 1) * 128], mb[:, hl, 0:n],
                                 hl == 0, hl == 3, ["Wout_m", ("mixh", mslot, hl)], [bank.key])
                          xupdate(l, 2, oc, bi, t0, n, bank)
                      items.append(item)
                  return items

              for it_ in qproj_items(qblocks[0], 0):
                  it_()
              prev_w = []
              for bidx, bi in enumerate(qblocks):
                  t0, n = BLK[bi]
                  kts = [0, 1] if bi == 0 else list(range(NT))
                  qset = bidx % 2
                  mslot = bidx % 2
                  mb = mixh[mslot]
                  side = list(prev_w)
                  if bidx + 1 < len(qblocks):
                      side += qproj_items(qblocks[bidx + 1], (bidx + 1) % 2)
                  jobs = []
                  npair = len(kts) // 2
                  for hl in range(4):
                      for pi_ in range(npair):
                          jobs.append(dict(hl=hl, pi=pi_, npair=npair, kts=kts[2 * pi_:2 * pi_ + 2],
                                           q_=qTh[qset * 4 + hl], qk_=("qTh", qset * 4 + hl)))
                  fins = []

                  def emit_S(jb):
                      c = pti_box[0]
                      pti_box[0] += 1
                      sp_ = ps2[c % 2]
                      spk = "b0" if c % 2 == 0 else "b2"
                      spk2 = "b1" if c % 2 == 0 else "b3"
                      pt = PT[c % 3]
                      ptk = ("PT", c % 3)
                      jb["pt"], jb["ptk"] = pt, ptk
                      hl_ = jb["hl"]
                      for j, kt in enumerate(jb["kts"]):
                          kbi = 0 if kt < 2 else 1 + (kt - 2) // 4
                          MM(sp_[:, j * 512:j * 512 + n], kTm[0:96, hl_, kt * 128:(kt + 1) * 128], jb["q_"][0:96, 0:n],
                             True, True, [("kTm", hl_, kbi), jb["qk_"]], [spk, spk2])
                      ACT(pt[:, :].rearrange("p (a i) -> p a i", a=2)[:, :, 0:n],
                          sp_[:, :].rearrange("p (a i) -> p a i", a=2)[:, :, 0:n], AF.Exp, [spk, spk2], [ptk], scale=SCALE_MLA)

                  fin_a_done = set()

                  def emit_fin_a(hl_, opar):
                      okey = "b4" if opar == 0 else "b5"
                      obank = ps1[opar]
                      rd, rdk = rden[opar], ("rden", opar)
                      ACT(rd[64:65, 0:n], obank[64:65, 0:n], AF.Ln, [okey], [rdk])
                      ACT(rd[64:65, 0:n], rd[64:65, 0:n], AF.Exp, [rdk], [rdk], scale=-1.0)
                      fin_a_done.add((hl_, opar))

                  def emit_fin(hl_, opar):
                      if (hl_, opar) not in fin_a_done:
                          emit_fin_a(hl_, opar)
                      okey = "b4" if opar == 0 else "b5"
                      obank = ps1[opar]
                      rd, rdk = rden[opar], ("rden", opar)
                      bb = next_bank([6, 7])
                      MM(bb.ap(0, 64, 0, n), onesf[64:65, 0:64], rd[64:65, 0:n], True, True, [rdk, "onesf"], [bb.key])
                      CP(bcs[opar][:, 0:n], bb.ap(0, 64, 0, n), [bb.key], [("bcs", opar)], eng="dve")
                      TT(mb[:, hl_, 0:n], obank[0:64, 0:n], bcs[opar][:, 0:n], ALU.mult, [okey, ("bcs", opar)],
                         [("mixh", mslot, hl_)])

                  def emit_PV(jb):
                      hl_, pi_ = jb["hl"], jb["pi"]
                      if pi_ == 0:
                          oi_box[0] += 1
                      opar = oi_box[0] % 2
                      if pi_ == 0:
                          for f_ in list(fins):
                              if f_[0][1] == opar:
                                  emit_fin(*f_[0])
                                  fins.remove(f_)
                      okey = "b4" if opar == 0 else "b5"
                      obank = ps1[opar]
                      pt, ptk = jb["pt"], jb["ptk"]
                      for j, kt in enumerate(jb["kts"]):
                          MM(obank[0:65, 0:n], vaug[:, kt, hl_, :], pt[:, j * 512:j * 512 + n],
                             (pi_ == 0 and j == 0), (pi_ == jb["npair"] - 1 and j == 1), [ptk, ("vaug", kt)], [okey])
                      if pi_ == jb["npair"] - 1:
                          return (hl_, opar)
                      return None

                  pend = []
                  for ji, jb in enumerate(jobs):
                      emit_S(jb)
                      pend.append(jb)
                      if len(pend) > 2:
                          r_ = emit_PV(pend.pop(0))
                          if r_ is not None:
                              fins.append([r_, 6])
                      for f_ in list(fins):
                          f_[1] -= 1
                          if f_[1] == 3 and f_[0] not in fin_a_done:
                              emit_fin_a(*f_[0])
                          if f_[1] <= 0:
                              emit_fin(*f_[0])
                              fins.remove(f_)
                      if side and ji % 2 == 1:
                          side.pop(0)()
                  while pend:
                      r_ = emit_PV(pend.pop(0))
                      if r_ is not None:
                          fins.append([r_, 0])
                  for f_ in fins:
                      emit_fin(*f_[0])
                  while side:
                      side.pop(0)()
                  prev_w = wout_items(bi, mslot)
              for it_ in prev_w:
                  it_()
          ar.release(m0)

          chk(8, l)
          m0 = ar.mark()
          chk(8, l)
          NSLOT = 3
          W1 = [ar.alloc("W1", [128, 8, 512], BF16) for _ in range(NSLOT)]
          W2 = [ar.alloc("W2", [128, 4, 1024], BF16) for _ in range(NSLOT)]
          rl = [ar.alloc("rl", [128, 512], F32) for _ in range(2)]
          h1 = [ar.alloc("h1", [128, 4, 512], BF16) for _ in range(2)]
          NORM_KEYS = [("sq", 0), ("sq", 1), ("rs", 0), ("rs", 1), ("tmpf", 0), ("tmpf", 1)]
          if l == 0:
              o0 = ar.off
              wada1 = [ar.alloc("wada", [128, 8, 1024], BF16) for _ in range(2)]
              o1 = o0 + 16384
              sq = [ar.alloc_at("sq", [128, 8, 512], BF16, o0 + 8192 * i) for i in range(2)]
              rs = [ar.alloc_at("rs", [128, 512], F32, o1 + 2048 * i) for i in range(2)]
              tmpf = [ar.alloc_at("tmpf", [128, 512], F32, o1 + 4096 + 2048 * i) for i in range(2)]
          else:
              sq = [ar.alloc("sq", [128, 8, 512], BF16) for _ in range(2)]
              rs = [ar.alloc("rs", [128, 512], F32) for _ in range(2)]
              tmpf = [ar.alloc("tmpf", [128, 512], F32) for _ in range(2)]
              ost1 = [ar.alloc("ost", [128, 512], F32) for _ in range(4)]
          FFN_BANKS = [0, 1, 2, 3, 4, 5] if l == 0 else [0, 1, 2, 3, 4, 5, 6]
          it = 0
          for g in range(8):
              if l == 0 and 1 <= g <= 6:
                  adaln_piece(1, g - 1, wada1, 6, alias=NORM_KEYS)
                  if g == 6:
                      adaln_A(1, 0)
                      adaln_A(1, 1)
              s = g % NSLOT
              fw.dma("pool", W1[s][:, :, :], w_ff1_d[l].rearrange("(k p) n -> p k n", p=128)[:, :, g * 512:(g + 1) * 512],
                     writes=[("W1", s)])
              fw.dma("pool", W2[s][:, :, :], w_ff2_d[l, g * 512:(g + 1) * 512, :].rearrange("(k p) n -> p k n", p=128),
                     writes=[("W2", s)])
              for bi in qblocks:
                  t0, n = BLK[bi]
                  if g == 0:
                      def _n2(b_):
                          norm_pass(l, lambda c, w: A2[:, l, c, w:w + 1], lambda c, w: mod(l, 3, c, w), [b_],
                                    lambda c, bi_, t0_, n_: (H[:, c, t0_:t0_ + n_], hk(bi_), (lambda c_, w_: mod(l, 3, c_, w_))),
                                    (sq, rs, tmpf), [("A", l, 1), ("modT", l, 3)])
                      qi_ = qblocks.index(bi)
                      if qi_ == 0:
                          _n2(bi)
                      if qi_ + 1 < len(qblocks):
                          _n2(qblocks[qi_ + 1])
                      chk(9, l)
                  hb = h1[it % 2]
                  hbk = ("h1", it % 2)
                  it += 1
                  for f in range(4):
                      bank = next_bank(FFN_BANKS)
                      for k in range(8):
                          MM(bank.ap(0, 128, 0, n), W1[s][:, k, f * 128:(f + 1) * 128], H[:, k, t0:t0 + n], k == 0, k == 7,
                             [("W1", s), hk(bi)], [bank.key])
                      r_ = rl[f % 2]
                      ACT(r_[:, 0:n], bank.ap(0, 128, 0, n), AF.Relu, [bank.key], [("rl", f % 2)])
                      TT(hb[:, f, 0:n], r_[:, 0:n], r_[:, 0:n], ALU.mult, [("rl", f % 2)], [(hbk, f)])
                  for oc in range(8):
                      bank = next_bank(FFN_BANKS)
                      for f in range(4):
                          MM(bank.ap(0, 128, 0, n), W2[s][:, f, oc * 128:(oc + 1) * 128], hb[:, f, 0:n], f == 0, f == 3,
                             [("W2", s), (hbk, f)], [bank.key])
                      xupdate(l, 5, oc, bi, t0, n, bank)
                  if l == 0 and g == 7:
                      def _n1(b_):
                          norm_pass(1, lambda c, w: A1[:, 1, c, w:w + 1], lambda c, w: mod(1, 0, c, w), [b_],
                                    lambda c, bi_, t0_, n_: (H[:, c, t0_:t0_ + n_], hk(bi_), (lambda c_, w_: mod(1, 0, c_, w_))),
                                    (sq, rs, tmpf), [("A", 1, 0), ("modT", 1, 0)])
                      qj_ = qblocks.index(bi)
                      if qj_ >= 1:
                          _n1(qblocks[qj_ - 1])
                      if qj_ == len(qblocks) - 1:
                          _n1(bi)
                  if l == 1 and g == 7:
                      qi_ = qblocks.index(bi)
                      if qi_ >= 1:
                          final_norm_block(qblocks[qi_ - 1], sq, rs, ost1)
                      if qi_ == len(qblocks) - 1:
                          final_norm_block(bi, sq, rs, ost1)
                      fin_state["done"] = True
          ar.release(m0)


    except _Stop:
        fw.barrier()
        ar.off = persist_mark

    if not fin_state["done"]:
        if stage < 99:
            ar.off = 16512 + 180000
        sq = [ar.alloc("sq", [128, 8, 512], BF16) for _ in range(2)]
        rs = [ar.alloc("rs", [128, 512], F32) for _ in range(2)]
        ost = [ar.alloc("ost", [128, 512], F32) for _ in range(4)]
        for bi in [1, 2, 3, 4]:
            final_norm_block(bi, sq, rs, ost)
    outs = fin_state["outs"]
    fw.finish("sp", outs)
    fw.emit()
    return nc


def _consts():
    L, GW = 2048, 64
    t = np.arange(L)
    row = (t // GW).astype(np.float32)
    col = (t % GW).astype(np.float32)

    def tables(dim):
        nf = dim // 4
        inv = (10000.0 ** (-np.arange(nf, dtype=np.float32) / nf)).astype(np.float32)
        ang = np.concatenate([row[:, None] * inv, col[:, None] * inv], axis=-1).astype(np.float32)
        return np.cos(ang).astype(np.float32), np.sin(ang).astype(np.float32)

    cr, sr = tables(64)
    cosD = np.concatenate([cr, cr], axis=-1).reshape(16, 128, 64).transpose(1, 0, 2).reshape(128, 16 * 64)
    sinS = np.concatenate([-sr, sr], axis=-1).reshape(16, 128, 64).transpose(1, 0, 2).reshape(128, 16 * 64)
    cm, sm = tables(32)
    ropeM = np.zeros((128, 2, L), np.float32)
    ropeM[64:96, 0, :] = np.concatenate([cm, cm], axis=-1).T
    ropeM[64:96, 1, :] = np.concatenate([-sm, sm], axis=-1).T
    bands = np.zeros((20, 128, 128), np.float32)
    for wi, w in enumerate((2, 4, 8, 16)):
        def mat(Lseq, tile_lo, src_lo):
            m = np.zeros((128, 128), np.float32)
            for tt in range(128):
                tg = tile_lo + tt
                lo = min(max(tg - w // 2, 0), Lseq)
                hi = min(max(tg - w // 2 + w, 0), Lseq)
                for sg in range(lo, hi):
                    sl = sg - src_lo
                    if 0 <= sl < 128:
                        m[sl, tt] += 1.0 / (hi - lo)
                sl = tg - src_lo
                if 0 <= sl < 128:
                    m[sl, tt] -= 1.0
            return m
        Ls = 1024
        bands[wi * 5 + 0] = mat(Ls, 256, 256)
        bands[wi * 5 + 1] = mat(Ls, 0, 0)
        bands[wi * 5 + 2] = mat(Ls, Ls - 128, Ls - 128)
        bands[wi * 5 + 3] = mat(Ls, 256, 128)
        bands[wi * 5 + 4] = mat(Ls, 256, 384)
    bands = bands.transpose(1, 0, 2).reshape(128, 20 * 128)
    j = np.arange(128, dtype=np.float32)[:, None]
    i = np.arange(128, dtype=np.float32)[None, :]
    mconst = np.stack([np.maximum(i - j, 0), np.maximum(j - i, 0), (i >= j) * 0.125, (j > i) * 0.125], axis=1)
    mconst = mconst.astype(np.float32).reshape(128, 4 * 128)
    efree = np.stack([np.broadcast_to(i + 1.0, (128, 128)), np.broadcast_to(128.0 - i, (128, 128))], axis=1)
    efree = np.ascontiguousarray(efree, dtype=np.float32).reshape(128, 2 * 128)
    ecol = np.concatenate([127.0 - j, j], axis=1).astype(np.float32)
    return dict(cosD=np.ascontiguousarray(cosD), sinS=np.ascontiguousarray(sinS), ropeM=ropeM.reshape(128, 2 * L),
                bands=np.ascontiguousarray(bands), mconst=mconst, efree=efree, ecol=ecol)


def _fm(v, lead):
    v = np.asarray(v, np.float32)
    a = v.reshape(lead, -1, 128)
    return np.ascontiguousarray(a.transpose(2, 0, 1).reshape(128, -1))


_NC_CACHE = {}


def kernel(x, c, ctx, c_ctx, w_ada, b_ada, norm_mix, w_in, q_norm, w_uq, kv_norm, w_ukv,
           ret_decay_logit, w_pool, pool_scale, w_out, norm_mlp, w_ff1, w_ff2, norm_final):
    f = lambda a: np.ascontiguousarray(np.asarray(a, dtype=np.float32))
    x, c, ctx, c_ctx = f(x), f(c), f(ctx), f(c_ctx)
    w_in = f(w_in)
    kpe = w_in[:, :, 1408:1440]
    w_in_ext = np.concatenate([w_in, kpe[:, :, 16:32], kpe[:, :, 0:16]], axis=2)
    wq = f(w_uq).reshape(2, 256, 8, 96)
    nope, rp = wq[..., 0:64], wq[..., 64:96]
    rp_sw = np.concatenate([rp[..., 16:32], rp[..., 0:16]], axis=-1)
    w_uq_ext = np.ascontiguousarray(np.concatenate([nope, rp, nope, rp_sw], axis=-1).reshape(2, 256, 1536))
    shared = dict(
        w_ada=f(w_ada), b_adaT=_fm(f(b_ada).reshape(2, 6144), 2), gmix=_fm(norm_mix, 2), gmlp=_fm(norm_mlp, 2),
        gfin=_fm(f(norm_final)[None], 1), w_in_ext=np.ascontiguousarray(w_in_ext), qnorm=_fm(q_norm, 2),
        kvnorm=_fm(kv_norm, 2), w_uq_ext=w_uq_ext, w_ukv=f(w_ukv),
        logit=np.ascontiguousarray(np.broadcast_to(f(ret_decay_logit).reshape(1, 16), (128, 16))),
        w_pool=f(w_pool), pscale=np.ascontiguousarray(f(pool_scale).reshape(2, 4, 64).transpose(2, 0, 1).reshape(64, 8)),
        w_out=f(w_out), w_ff1=f(w_ff1), w_ff2=f(w_ff2))
    shared.update(_consts())
    in_maps = []
    for b in range(8):
        xT = np.ascontiguousarray(np.concatenate([ctx[b].T, x[b].T], axis=1))
        cc = np.stack([c[b].reshape(8, 128).T, c_ctx.reshape(8, 128).T], axis=2).reshape(128, 16)
        m = dict(shared)
        m["xT"] = xT
        m["cc"] = np.ascontiguousarray(cc)
        in_maps.append(m)
    if _NC_CACHE.get("prep_only"):
        return in_maps
    if "nc" not in _NC_CACHE:
        _NC_CACHE["nc"] = build_nc()
    used = _NC_CACHE["nc"]._used_inputs
    in_maps = [{k: v for k, v in m.items() if k in used} for m in in_maps]
    res = run_bass_kernel_spmd(_NC_CACHE["nc"], in_maps, core_ids=list(range(8)))
    out = np.stack([np.ascontiguousarray(r["outT"].T) for r in res.results], axis=0)
    return out.astype(np.float32)
```
